# Optimizing a Trainium2 kernel written in Bass

```python
import math
import jax, jax.numpy as jnp
from jax import lax
import numpy as np


D_MODEL = 1024
BATCH = 16
SEQ = 4096
DEPTH = 4
DEC_BATCH = 2
DEC_SEQ = 16384
PAST_LEN = 128

N_MIXERS = 3
GRID_W = 64
HEAD_DIM = 64
D_FF = 2816
EPS = 1e-6
Q_BLOCK = 128
NEG_INF = -1e30
A_HEADS = D_MODEL // (2 * HEAD_DIM)
A_VDIM = 2 * HEAD_DIM
T5_BUCKETS = 32
T5_MAX_DIST = 128
B_HEADS = D_MODEL // HEAD_DIM
B_KV_HEADS = B_HEADS // 4
B_GROUP = B_HEADS // B_KV_HEADS
ROPE_THETA = 10000.0
C_HEADS = D_MODEL // HEAD_DIM
NA_ROWS_MAX = 8
NA_COLS = 16

kernel_name = 'hybrid_bidir_encoder_interleaved'


def _layer_counts():
    n_a = len(range(0, DEPTH, N_MIXERS))
    n_b = len(range(1, DEPTH, N_MIXERS))
    n_c = len(range(2, DEPTH, N_MIXERS))
    return n_a, n_b, n_c


def rms_norm(x, g):
    xf = x.astype(jnp.float32)
    y = xf * lax.rsqrt(jnp.mean(xf * xf, axis=-1, keepdims=True) + EPS)
    return (y * g.astype(jnp.float32)).astype(x.dtype)


def swiglu(x, w_in, w_out):
    gate, up = jnp.split(x @ w_in, 2, axis=-1)
    return (jax.nn.silu(gate) * up) @ w_out


def t5_bucket(rel):
    nb = T5_BUCKETS // 2
    max_exact = nb // 2
    ret = jnp.where(rel > 0, nb, 0)
    n = jnp.abs(rel)
    n_f = jnp.maximum(n, 1).astype(jnp.float32)
    large = max_exact + (jnp.log(n_f / max_exact) / math.log(T5_MAX_DIST / max_exact) * (nb - max_exact)).astype(jnp.int32)
    large = jnp.minimum(large, nb - 1)
    return ret + jnp.where(n < max_exact, n, large)


def diff_attention(h, w_qkv, lam, subln, w_out, t5_table, lambda_init):
    b, s_len, _ = h.shape
    q, k, v = jnp.split(h @ w_qkv, 3, axis=-1)
    nblk = s_len // Q_BLOCK
    q = q.reshape(b, nblk, Q_BLOCK, A_HEADS, 2, HEAD_DIM).transpose(1, 0, 2, 3, 4, 5)
    k = k.reshape(b, s_len, A_HEADS, 2, HEAD_DIM)
    v = v.reshape(b, s_len, A_HEADS, A_VDIM)
    lamf = lam.astype(jnp.float32)
    lam_full = jnp.exp(jnp.sum(lamf[0] * lamf[1])) - jnp.exp(jnp.sum(lamf[2] * lamf[3])) + lambda_init
    offs = jnp.arange(-(s_len - 1), s_len, dtype=jnp.int32)
    bias_vec = t5_table.astype(jnp.float32)[t5_bucket(offs)]
    kpos = jnp.arange(s_len, dtype=jnp.int32)
    scale = HEAD_DIM ** -0.5

    def block(args):
        qblk, start = args
        sc = jnp.einsum('bqhid,bkhid->bhiqk', qblk, k).astype(jnp.float32) * scale
        qpos = start + jnp.arange(Q_BLOCK, dtype=jnp.int32)
        bias = bias_vec[kpos[None, :] - qpos[:, None] + (s_len - 1)]
        sc = sc + jnp.transpose(bias, (2, 0, 1))[:, None]
        p = jax.nn.softmax(sc, axis=-1)
        attn = p[:, :, 0] - lam_full * p[:, :, 1]
        return jnp.einsum('bhqk,bkhe->bqhe', attn.astype(v.dtype), v)

    o = lax.map(block, (q, jnp.arange(nblk, dtype=jnp.int32) * Q_BLOCK))
    o = o.transpose(1, 0, 2, 3, 4).reshape(b, s_len, A_HEADS, A_VDIM)
    o = rms_norm(o, subln) * (1.0 - lambda_init)
    return o.reshape(b, s_len, D_MODEL) @ w_out


def _rope_half(x, pos):
    n = x.shape[-1] // 2
    freqs = ROPE_THETA ** (-jnp.arange(n, dtype=jnp.float32) / n)
    ang = pos.astype(jnp.float32)[:, None] * freqs[None, :]
    cos = jnp.cos(ang)[None, :, None, :]
    sin = jnp.sin(ang)[None, :, None, :]
    x1, x2 = x[..., :n], x[..., n:]
    return jnp.concatenate([x1 * cos - x2 * sin, x1 * sin + x2 * cos], axis=-1)


def axial_rope(x, row, col):
    xf = x.astype(jnp.float32)
    half = HEAD_DIM // 2
    return jnp.concatenate([_rope_half(xf[..., :half], row), _rope_half(xf[..., half:], col)], axis=-1).astype(x.dtype)


def gqa_axial(h, w_qkv, q_norm, k_norm, w_out):
    b, s_len, _ = h.shape
    qkv = h @ w_qkv
    nq = B_HEADS * HEAD_DIM
    nk = B_KV_HEADS * HEAD_DIM
    q = qkv[..., :nq].reshape(b, s_len, B_HEADS, HEAD_DIM)
    k = qkv[..., nq:nq + nk].reshape(b, s_len, B_KV_HEADS, HEAD_DIM)
    v = qkv[..., nq + nk:].reshape(b, s_len, B_KV_HEADS, HEAD_DIM)
    t = jnp.arange(s_len, dtype=jnp.int32)
    row, col = t // GRID_W, t % GRID_W
    q = axial_rope(rms_norm(q, q_norm), row, col)
    k = axial_rope(rms_norm(k, k_norm), row, col)
    nblk = s_len // Q_BLOCK
    q = q.reshape(b, nblk, Q_BLOCK, B_KV_HEADS, B_GROUP, HEAD_DIM).transpose(1, 0, 2, 3, 4, 5)
    scale = HEAD_DIM ** -0.5

    def block(qblk):
        sc = jnp.einsum('bqgrd,bkgd->bgrqk', qblk, k).astype(jnp.float32) * scale
        p = jax.nn.softmax(sc, axis=-1)
        return jnp.einsum('bgrqk,bkgd->bqgrd', p.astype(v.dtype), v)

    o = lax.map(block, q)
    o = o.transpose(1, 0, 2, 3, 4, 5).reshape(b, s_len, D_MODEL)
    return o @ w_out


def neighbourhood_attention(h, w_qkv, rpb, w_out):
    b, s_len, _ = h.shape
    rows = s_len // GRID_W
    kr = min(NA_ROWS_MAX, rows)
    q, k, v = jnp.split(h @ w_qkv, 3, axis=-1)
    q = q.reshape(b, rows, GRID_W, C_HEADS, HEAD_DIM)
    k = k.reshape(b, rows, GRID_W, C_HEADS, HEAD_DIM)
    v = v.reshape(b, rows, GRID_W, C_HEADS, HEAD_DIM)
    col = jnp.arange(GRID_W, dtype=jnp.int32)
    c_start = jnp.clip(col - NA_COLS // 2, 0, GRID_W - NA_COLS)
    col_mask = (col[None, :] >= c_start[:, None]) & (col[None, :] < c_start[:, None] + NA_COLS)
    col_idx = jnp.clip(col[None, :] - col[:, None] + NA_COLS - 1, 0, 2 * NA_COLS - 2)
    rpbf = rpb.astype(jnp.float32)
    scale = HEAD_DIM ** -0.5

    def row_step(r):
        rs = jnp.clip(r - kr // 2, 0, rows - kr)
        kw = lax.dynamic_slice_in_dim(k, rs, kr, axis=1)
        vw = lax.dynamic_slice_in_dim(v, rs, kr, axis=1)
        qr = lax.dynamic_index_in_dim(q, r, axis=1, keepdims=False)
        sc = jnp.einsum('bqhd,bikhd->bhqik', qr, kw).astype(jnp.float32) * scale
        row_idx = rs + jnp.arange(kr, dtype=jnp.int32) - r + NA_ROWS_MAX - 1
        bias = rpbf[:, row_idx][:, :, col_idx]
        sc = sc + jnp.transpose(bias, (0, 2, 1, 3))[None]
        sc = jnp.where(col_mask[None, None, :, None, :], sc, NEG_INF)
        p = jax.nn.softmax(sc.reshape(b, C_HEADS, GRID_W, kr * GRID_W), axis=-1)
        p = p.reshape(b, C_HEADS, GRID_W, kr, GRID_W)
        return jnp.einsum('bhqik,bikhd->bqhd', p.astype(vw.dtype), vw)

    o = lax.map(row_step, jnp.arange(rows, dtype=jnp.int32))
    o = o.transpose(1, 0, 2, 3, 4).reshape(b, s_len, D_MODEL)
    return o @ w_out


def encoder_trunk(x, ffn_norm, ffn_w_in, ffn_w_out, mix_norm, a_w_qkv, a_lambda, a_subln, a_w_out,
                  t5_table, b_w_qkv, b_q_norm, b_k_norm, b_w_out, c_w_qkv, c_rpb, c_w_out, final_norm):
    for i in range(DEPTH):
        j = i // N_MIXERS
        kind = i % N_MIXERS
        x = x + 0.5 * swiglu(rms_norm(x, ffn_norm[i, 0]), ffn_w_in[i, 0], ffn_w_out[i, 0])
        h = rms_norm(x, mix_norm[i])
        if kind == 0:
            lambda_init = 0.8 - 0.6 * math.exp(-0.3 * i)
            m = diff_attention(h, a_w_qkv[j], a_lambda[j], a_subln[j], a_w_out[j], t5_table, lambda_init)
        elif kind == 1:
            m = gqa_axial(h, b_w_qkv[j], b_q_norm[j], b_k_norm[j], b_w_out[j])
        else:
            m = neighbourhood_attention(h, c_w_qkv[j], c_rpb[j], c_w_out[j])
        x = x + m
        x = x + 0.5 * swiglu(rms_norm(x, ffn_norm[i, 1]), ffn_w_in[i, 1], ffn_w_out[i, 1])
    return rms_norm(x, final_norm)


def setup_inputs(seed: int = 0) -> dict:
    key = jax.random.key(seed)
    ks = jax.random.split(key, 24)
    n_a, n_b, n_c = _layer_counts()
    f32 = jnp.float32
    def w(k, shape, fan_in):
        return jax.random.normal(k, shape, f32) * (fan_in ** -0.5)
    def gain(k, shape):
        return 1.0 + 0.01 * jax.random.normal(k, shape, f32)
    return {
        'x_prompt': jax.random.normal(ks[0], (BATCH, SEQ, D_MODEL), f32),
        'x_sample': jax.random.normal(ks[1], (DEC_BATCH, DEC_SEQ, D_MODEL), f32),
        'ffn_norm': gain(ks[2], (DEPTH, 2, D_MODEL)),
        'ffn_w_in': w(ks[3], (DEPTH, 2, D_MODEL, 2 * D_FF), D_MODEL),
        'ffn_w_out': w(ks[4], (DEPTH, 2, D_FF, D_MODEL), D_FF),
        'mix_norm': gain(ks[5], (DEPTH, D_MODEL)),
        'a_w_qkv': w(ks[6], (n_a, D_MODEL, 3 * D_MODEL), D_MODEL),
        'a_lambda': 0.1 * jax.random.normal(ks[7], (n_a, 4, HEAD_DIM), f32),
        'a_subln': gain(ks[8], (n_a, A_VDIM)),
        'a_w_out': w(ks[9], (n_a, D_MODEL, D_MODEL), D_MODEL),
        't5_table': 0.1 * jax.random.normal(ks[10], (T5_BUCKETS, A_HEADS), f32),
        'b_w_qkv': w(ks[11], (n_b, D_MODEL, (B_HEADS + 2 * B_KV_HEADS) * HEAD_DIM), D_MODEL),
        'b_q_norm': gain(ks[12], (n_b, HEAD_DIM)),
        'b_k_norm': gain(ks[13], (n_b, HEAD_DIM)),
        'b_w_out': w(ks[14], (n_b, D_MODEL, D_MODEL), D_MODEL),
        'c_w_qkv': w(ks[15], (n_c, D_MODEL, 3 * D_MODEL), D_MODEL),
        'c_rpb': 0.1 * jax.random.normal(ks[16], (n_c, C_HEADS, 2 * NA_ROWS_MAX - 1, 2 * NA_COLS - 1), f32),
        'c_w_out': w(ks[17], (n_c, D_MODEL, D_MODEL), D_MODEL),
        'final_norm': gain(ks[18], (D_MODEL,)),
    }


def reference(x_prompt, x_sample, ffn_norm, ffn_w_in, ffn_w_out, mix_norm, a_w_qkv, a_lambda, a_subln, a_w_out,
              t5_table, b_w_qkv, b_q_norm, b_k_norm, b_w_out, c_w_qkv, c_rpb, c_w_out, final_norm):
    y_prompt = encoder_trunk(x_prompt, ffn_norm, ffn_w_in, ffn_w_out, mix_norm, a_w_qkv, a_lambda, a_subln, a_w_out,
                             t5_table, b_w_qkv, b_q_norm, b_k_norm, b_w_out, c_w_qkv, c_rpb, c_w_out, final_norm)
    y_sample = encoder_trunk(x_sample, ffn_norm, ffn_w_in, ffn_w_out, mix_norm, a_w_qkv, a_lambda, a_subln, a_w_out,
                             t5_table, b_w_qkv, b_q_norm, b_k_norm, b_w_out, c_w_qkv, c_rpb, c_w_out, final_norm)
    return (y_prompt, y_sample)
```

```python
import math
from contextlib import ExitStack

import numpy as np
import jax
import jax.numpy as jnp

import concourse.bass as bass
import concourse.mybir as mybir
from concourse.bass_utils import run_bass_kernel_spmd

F32 = mybir.dt.float32
BF16 = mybir.dt.bfloat16
AF = mybir.ActivationFunctionType
ALU = mybir.AluOpType

P = 128
D = 1024
DFF = 2816
NKC = 8
NJ = 22
TT = 512
EPS = 1e-6
GRID_W = 64
NEG = -30000.0


class Sem:
    def __init__(self, nc, name):
        self.h = nc.alloc_semaphore(name)
        self.v = 0

    def inc(self, ins, n=1):
        ins.then_inc(self.h, n)
        self.v += n
        return self.v


def default_cfg():
    return dict(SP=4096, SS=16384, NPC=2, depth=4)


def t5_bucket_np(rel):
    rel = jnp.asarray(rel, dtype=jnp.int32)
    nb = 16
    max_exact = 8
    ret = jnp.where(rel > 0, nb, 0)
    n = jnp.abs(rel)
    n_f = jnp.maximum(n, 1).astype(jnp.float32)
    large = max_exact + (jnp.log(n_f / max_exact) / math.log(128 / max_exact) * (nb - max_exact)).astype(jnp.int32)
    large = jnp.minimum(large, nb - 1)
    return np.asarray(ret + jnp.where(n < max_exact, n, large))


def make_consts(cfg):
    with jax.default_device(jax.devices("cpu")[0]):
        return _make_consts(cfg)


def _make_consts(cfg):
    SS = cfg["SS"]
    c = {}
    c["ident"] = np.eye(P, dtype=np.float32)
    rel = np.arange(-255, 256)
    bk = t5_bucket_np(rel)
    oh = np.zeros((32, 511 + 256), np.float32)
    oh[bk, np.arange(511)] = 1.0
    bm = int(t5_bucket_np(np.array([-100000]))[0])
    bp = int(t5_bucket_np(np.array([100000]))[0])
    assert int(t5_bucket_np(np.array([-129]))[0]) == bm and int(t5_bucket_np(np.array([129]))[0]) == bp
    oh[bm, 511:511 + 128] = 1.0
    oh[bp, 511 + 128:] = 1.0
    c["t5oh"] = oh
    blk = np.zeros((P, P), np.float32)
    blk[:64, :64] = 1.0 / 64
    blk[64:, 64:] = 1.0 / 64
    c["blk64"] = blk
    sw = np.zeros((P, P), np.float32)
    for p in range(P):
        f32 = p % 32
        partner = p + 16 if f32 < 16 else p - 16
        sw[partner, p] = 1.0
    c["swap"] = sw
    t = jnp.arange(SS, dtype=jnp.int32)
    row, col = t // GRID_W, t % GRID_W
    freqs = 10000.0 ** (-jnp.arange(16, dtype=jnp.float32) / 16)
    ang_r = row.astype(jnp.float32)[:, None] * freqs[None, :]
    ang_c = col.astype(jnp.float32)[:, None] * freqs[None, :]
    cr, sr = np.asarray(jnp.cos(ang_r)), np.asarray(jnp.sin(ang_r))
    cc, sc = np.asarray(jnp.cos(ang_c)), np.asarray(jnp.sin(ang_c))
    rope = np.zeros((2, P, SS), np.float32)
    for p in range(P):
        f = p % 64
        i = f % 16
        sgn = -1.0 if (f % 32) < 16 else 1.0
        if f < 32:
            rope[0, p] = cr[:, i]
            rope[1, p] = sgn * sr[:, i]
        else:
            rope[0, p] = cc[:, i]
            rope[1, p] = sgn * sc[:, i]
    c["rope"] = rope
    colq = np.arange(64)
    cstart = np.clip(colq - 8, 0, 64 - 16)
    naoh = np.zeros((32, 64, P), np.float32)
    for qc in range(64):
        for kc in range(64):
            valid = (kc >= cstart[qc]) and (kc < cstart[qc] + 16)
            if valid:
                ci = int(np.clip(kc - qc + 15, 0, 30))
                naoh[ci, qc, kc] = 1.0
                naoh[ci, qc, kc + 64] = 1.0
            else:
                naoh[31, qc, kc] = 1.0
                naoh[31, qc, kc + 64] = 1.0
    c["naoh"] = naoh
    return c


def na_tile_plan(S):
    R = S // GRID_W
    kr = min(8, R)
    T = S // TT
    tiles = {}
    plan = []
    for t in range(T):
        lst = []
        for j in range(8):
            kc = 4 * t - 2 + j
            if kc < 0 or kc >= S // P:
                continue
            blocks = []
            anyvalid = False
            for a in range(2):
                krow = 2 * kc + a
                for b in range(8):
                    r = 8 * t + b
                    rs = int(np.clip(r - kr // 2, 0, R - kr))
                    if rs <= krow < rs + kr:
                        blocks.append(krow - r + 7)
                        anyvalid = True
                    else:
                        blocks.append(15)
            if not anyvalid:
                continue
            key = tuple(blocks)
            if key not in tiles:
                tiles[key] = len(tiles)
            lst.append((kc, tiles[key]))
        plan.append(lst)
    table = [None] * len(tiles)
    for k, v in tiles.items():
        table[v] = k
    return plan, table


class Builder:
    def __init__(self, cfg):
        self.cfg = cfg
        self.SP, self.SS, self.NPC, self.depth = cfg["SP"], cfg["SS"], cfg["NPC"], cfg["depth"]
        self.SC = self.SS // 4
        self.NPT = self.NPC * self.SP
        self.NTOK = self.NPT + self.SC
        self.nc = bass.Bass("TRN2", target_bir_lowering=False)
        self.nsem = 0
        self.st_all = None
        self.engs = None

    def sem(self, name):
        self.nsem += 1
        return Sem(self.nc, f"{name}_{self.nsem}")

    def din(self, name, shape, dt=F32):
        return self.nc.dram_tensor(name, list(shape), dt, kind="ExternalInput").ap()

    def dscr(self, name, shape, dt):
        return self.nc.dram_tensor(name, list(shape), dt).ap()

    def declare(self):
        nc = self.nc
        dep = self.depth
        self.x_in = self.din("x_in", [self.NTOK, D])
        self.ffn_norm = self.din("ffn_norm", [dep * 2, D])
        self.ffn_w_in = self.din("ffn_w_in", [dep * 2, D, 2 * DFF])
        self.ffn_w_out = self.din("ffn_w_out", [dep * 2, DFF, D])
        self.mix_norm = self.din("mix_norm", [dep, D])
        self.final_norm = self.din("final_norm", [1, D])
        self.a_w_qkv = self.din("a_w_qkv", [2, D, 3072])
        self.a_w_qkv_s = self.din("a_w_qkv_s", [2, D, 768])
        self.a_w_out = self.din("a_w_out", [2, D, D])
        self.a_w_out_s = self.din("a_w_out_s", [2, 256, D])
        self.a_lambda = self.din("a_lambda", [2, 256])
        self.a_subln = self.din("a_subln", [2, 128])
        self.t5_all = self.din("t5_all", [32, 10])
        self.b_w_qkv = self.din("b_w_qkv", [D, 2048])
        self.b_w_qkv_s = self.din("b_w_qkv_s", [D, 512])
        self.b_qk_norm = self.din("b_qk_norm", [2, 128])
        self.b_w_out = self.din("b_w_out", [D, D])
        self.b_w_out_s = self.din("b_w_out_s", [256, D])
        self.c_w_qkv = self.din("c_w_qkv", [D, 3072])
        self.c_w_qkv_s = self.din("c_w_qkv_s", [D, 768])
        self.c_rpbT = self.din("c_rpbT", [32, 20 * 16])
        self.c_w_out = self.din("c_w_out", [D, D])
        self.c_w_out_s = self.din("c_w_out_s", [256, D])
        self.k_ident = self.din("k_ident", [P, P])
        self.k_t5oh = self.din("k_t5oh", [32, 767])
        self.k_blk64 = self.din("k_blk64", [P, P])
        self.k_swap = self.din("k_swap", [P, P])
        self.k_rope = self.din("k_rope", [2, P, self.SS])
        self.k_naoh = self.din("k_naoh", [32, 64, P])
        self.y_out = nc.dram_tensor("y_out", [self.NTOK, D], F32, kind="ExternalOutput").ap()
        self.xres = self.dscr("xres", [self.NTOK, D], F32)
        self.hp = self.dscr("hp", [D, self.NPT], BF16)
        self.KP = self.SC // TT
        self.hs_in = [self.dscr(f"hs_in{k}", [D, TT], BF16) for k in range(self.KP)]
        self.hs_all = [self.dscr(f"hs_all{k}", [4 * D, TT], BF16) for k in range(self.KP)]
        self.qkT_p = self.dscr("qkT_p", [16, P, self.NPT], BF16)
        self.v_p = self.dscr("v_p", [self.NPT, D], BF16)
        self.qkT_s = self.dscr("qkT_s", [4, P, self.SS], BF16)
        self.v_s = self.dscr("v_s", [self.SS, 256], BF16)
        self.ys_in = [self.dscr(f"ys_in{k}", [4 * TT, D], F32) for k in range(self.KP)]
        self.ys_out = [self.dscr(f"ys_out{k}", [TT, D], F32) for k in range(self.KP)]
        self.t5tiles = self.dscr("t5tiles", [3, 10, P, P], F32)
        self.na_plan_p, self.na_table_p = na_tile_plan(self.SP)
        self.na_plan_s, self.na_table_s = na_tile_plan(self.SS)
        self.na_bias_p = self.dscr("na_bias_p", [16, len(self.na_table_p), P, TT], BF16)
        self.na_bias_s = self.dscr("na_bias_s", [4, len(self.na_table_s), P, TT], BF16)

    def phase_begin(self):
        nc = self.nc
        for e in (nc.sync, nc.gpsimd, nc.scalar, nc.vector, nc.tensor):
            e.wait_ge(self.st_all.h, self.st_all.v)
            if self.cc.v:
                e.wait_ge(self.cc.h, self.cc.v)

    def store(self, eng, out, in_):
        return self.st_all.inc(eng.dma_start(out=out, in_=in_), 16)

    def setup_consts(self, es):
        nc = self.nc
        self.ident = es.enter_context(self.sbt("ident", [P, P], BF16))
        self.ones_b = es.enter_context(self.sbt("ones_b", [P, P], BF16))
        self.onesm = es.enter_context(self.sbt("onesm", [P, P], BF16))
        self.epsc = es.enter_context(self.sbt("epsc", [P, 1], F32))
        self.ones_f = es.enter_context(self.sbt("ones_f", [P, P], F32))
        s = self.sem("cst")
        s.inc(nc.gpsimd.dma_start(out=self.ident[:], in_=self.k_ident[:, :]), 16)
        nc.vector.memset(self.ones_b[:], 1.0)
        nc.vector.memset(self.onesm[:], 1.0 / 128)
        nc.vector.memset(self.ones_f[:], 1.0)
        c = self.sem("cstv")
        c.inc(nc.vector.memset(self.epsc[:], EPS))
        for e in (nc.tensor, nc.scalar, nc.vector, nc.gpsimd, nc.sync):
            e.wait_ge(s.h, s.v)
            e.wait_ge(c.h, c.v)

    def t5_setup(self):
        nc = self.nc
        self.phase_begin()
        with ExitStack() as es:
            oh = es.enter_context(self.sbt("t5oh", [32, 767], F32))
            tb = es.enter_context(self.sbt("t5tb", [32, 10], F32))
            stg = [es.enter_context(self.sbt(f"t5stg{i}", [P, 10, 64], F32)) for i in range(2)]
            ps = [es.enter_context(self.pst(f"t5ps{i}", [P, 64, 8], F32)) for i in range(2)]
            ps2 = [es.enter_context(self.pst(f"t5pb{i}", [P, 64, 2], F32)) for i in range(2)]
            ld = self.sem("t5ld")
            ld.inc(nc.sync.dma_start(out=oh[:], in_=self.k_t5oh[:, :]), 16)
            ld.inc(nc.sync.dma_start(out=tb[:], in_=self.t5_all[:, :]), 16)
            nc.tensor.wait_ge(ld.h, ld.v)
            s_mm = self.sem("t5mm")
            s_ev = self.sem("t5ev")
            st = [self.sem("t5st0"), self.sem("t5st1")]
            it = 0
            for d in (-1, 0, 1):
                for qh in range(2):
                    b = it % 2
                    if it >= 2:
                        nc.tensor.wait_ge(s_ev.h, s_ev.base + it - 1)
                    for qi in range(64):
                        q = qh * 64 + qi
                        base = 128 * d + 255 - q
                        nc.tensor.matmul(ps[b][:, qi, :], oh[:, base:base + 128], tb[:, 0:8], start=True, stop=True)
                        mm = nc.tensor.matmul(ps2[b][:, qi, :], oh[:, base:base + 128], tb[:, 8:10], start=True, stop=True)
                    s_mm.inc(mm)
                    nc.vector.wait_ge(s_mm.h, s_mm.base + it + 1)
                    if it >= 2:
                        nc.vector.wait_ge(st[b].h, st[b].base + 16 * (it // 2))
                    nc.vector.tensor_copy(out=stg[b][:, 0:8, :], in_=ps[b][:, :, :].rearrange("p q h -> p h q"))
                    s_ev.inc(nc.vector.tensor_copy(out=stg[b][:, 8:10, :], in_=ps2[b][:, :, :].rearrange("p q h -> p h q")))
                    nc.gpsimd.wait_ge(s_ev.h, s_ev.base + it + 1)
                    ins = nc.gpsimd.dma_start(out=self.t5tiles[d + 1].rearrange("h k q -> k h q")[:, :, qh * 64:(qh + 1) * 64], in_=stg[b][:])
                    st[b].inc(ins, 16)
                    it += 1
            for b in range(2):
                for e in (nc.gpsimd, nc.vector, nc.tensor, nc.sync, nc.scalar):
                    e.wait_ge(st[b].h, st[b].v)

    def na_setup(self):
        nc = self.nc
        self.phase_begin()
        with ExitStack() as es:
            oh = es.enter_context(self.sbt("naoh", [32, 64, P], F32))
            rp = es.enter_context(self.sbt("narp", [32, 320], F32))
            mt = es.enter_context(self.sbt("namt", [P, 20, 16, 64], BF16))
            bt = [es.enter_context(self.sbt(f"nabt{i}", [P, TT], BF16)) for i in range(4)]
            ps = [es.enter_context(self.pst(f"naps{i}", [P, 320], F32)) for i in range(2)]
            ld = self.sem("nald")
            ld.inc(nc.sync.dma_start(out=oh[:], in_=self.k_naoh[:, :, :]), 16)
            ld.inc(nc.sync.dma_start(out=rp[:], in_=self.c_rpbT[:, :]), 16)
            nc.tensor.wait_ge(ld.h, ld.v)
            s_mm = self.sem("namm")
            s_ev = self.sem("naev")
            for qc in range(64):
                b = qc % 2
                if qc >= 2:
                    nc.tensor.wait_ge(s_ev.h, s_ev.base + qc - 1)
                s_mm.inc(nc.tensor.matmul(ps[b][:, :], oh[:, qc, :], rp[:, :], start=True, stop=True))
                nc.vector.wait_ge(s_mm.h, s_mm.base + qc + 1)
                s_ev.inc(nc.vector.tensor_copy(out=mt[:, :, :, qc], in_=ps[b][:, :].rearrange("p (h r) -> p h r", r=16)))
            s_ms = self.sem("nams")
            nc.vector.wait_ge(s_ev.h, s_ev.base + 64)
            s_ms.inc(nc.vector.memset(mt[:, :, 15, :], NEG))
            nc.vector.wait_ge(s_ms.h, s_ms.base + 1)
            s_bt = self.sem("nabt")
            st = [self.sem(f"nast{i}") for i in range(4)]
            it = 0
            for (nh, h0, table, dst) in ((16, 0, self.na_table_p, self.na_bias_p), (4, 16, self.na_table_s, self.na_bias_s)):
                for h in range(nh):
                    for ti, blocks in enumerate(table):
                        b = it % 4
                        if it >= 4:
                            nc.vector.wait_ge(st[b].h, st[b].base + 16 * (it // 4))
                        for a in range(2):
                            for bb in range(8):
                                ri = blocks[a * 8 + bb]
                                ins = nc.vector.tensor_copy(out=bt[b][64 * a:64 * a + 64, bb * 64:(bb + 1) * 64],
                                                            in_=mt[64 * a:64 * a + 64, h0 + h, ri, :])
                        s_bt.inc(ins)
                        nc.gpsimd.wait_ge(s_bt.h, s_bt.base + it + 1)
                        st[b].inc(nc.gpsimd.dma_start(out=dst[h, ti], in_=bt[b][:]), 16)
                        it += 1
            for b in range(4):
                for e in (nc.gpsimd, nc.vector, nc.tensor, nc.sync, nc.scalar):
                    e.wait_ge(st[b].h, st[b].v)

    def tile_order(self, sample_first=True):
        npt = self.NPT // TT
        nst = self.SC // TT
        pt = list(range(npt))
        stl = list(range(npt, npt + nst))
        return (stl + pt) if sample_first else (pt + stl)

    def ffn_phase(self, idx, src, post, li):
        nc = self.nc
        self.phase_begin()
        PE, ACT, DVE, POOL, SP = nc.tensor, nc.scalar, nc.vector, nc.gpsimd, nc.sync
        with ExitStack() as es:
            sb = lambda n, s, d: es.enter_context(self.sbt(n, s, d))
            pb = lambda n, s, d: es.enter_context(self.pst(n, s, d))
            win = sb("win", [P, NKC, 2 * DFF], BF16)
            wout = sb("wout", [P, NJ, D], BF16)
            gb1 = sb("gb1", [P, D], F32)
            gb2 = sb("gb2", [P, D], F32) if post else None
            xin = [sb(f"xin{i}", [P, D], F32) for i in range(2)]
            xsb = [sb(f"xsb{i}", [P, D], BF16) for i in range(4)]
            hT = sb("hT", [P, NKC, TT], BF16)
            actT = sb("actT", [P, NJ, TT], BF16)
            sg = sb("sg", [P, TT], F32)
            xr = [sb(f"xr{i}", [P, D], F32) for i in range(2)]
            ss = sb("ss", [P, 16], F32)
            if post == "mix":
                xsb2 = [sb(f"xsb2{i}", [P, D], BF16) for i in range(2)]
                h2st = [sb(f"h2st{i}", [P, NKC, P], BF16) for i in range(2)]
            tpA = pb("tpA", [P, D], BF16)
            tpB = pb("tpB", [P, D], BF16)
            G = [pb(f"G{i}", [P, TT], F32) for i in range(2)]
            U = [pb(f"U{i}", [P, TT], F32) for i in range(2)]
            Y = [pb(f"Y{i}", [P, TT], F32) for i in range(2)]

            wl = self.sem("wl")
            wsrc = self.ffn_w_in[idx].rearrange("(k p) n -> p k n", p=P)
            for k in range(NKC):
                wl.inc(POOL.dma_start(out=win[:, k, :], in_=wsrc[:, k, :]), 16)
            wsrc2 = self.ffn_w_out[idx].rearrange("(j p) n -> p j n", p=P)
            for j in range(NJ):
                wl.inc(POOL.dma_start(out=wout[:, j, :], in_=wsrc2[:, j, :]), 16)
            wlh = self.sem("wlh")
            wlh.inc(SP.dma_start(out=gb1[:], in_=self.ffn_norm[idx:idx + 1, :].partition_broadcast(P)), 16)
            if post == "mix":
                wlh.inc(SP.dma_start(out=gb2[:], in_=self.mix_norm[li:li + 1, :].partition_broadcast(P)), 16)
            elif post == "final":
                wlh.inc(SP.dma_start(out=gb2[:], in_=self.final_norm[0:1, :].partition_broadcast(P)), 16)
            for e in (PE, DVE):
                e.wait_ge(wl.h, wl.v)
                e.wait_ge(wlh.h, wlh.v)

            ld_x = [self.sem("ldx0"), self.sem("ldx1")]
            ld_xr = [self.sem("ldxr0"), self.sem("ldxr1")]
            st_x = [self.sem("stx0"), self.sem("stx1")]
            st_h2 = [self.sem("sth0"), self.sem("sth1")]
            s_ss, s_xsb, s_tp, s_ht = self.sem("ss"), self.sem("xsb"), self.sem("tp"), self.sem("ht")
            s_gu, s_sg, s_act = self.sem("gu"), self.sem("sg"), self.sem("act")
            s_y, s_res = self.sem("y"), self.sem("res")
            s_ss2, s_xsb2, s_tp2, s_h2 = self.sem("ss2"), self.sem("xsb2"), self.sem("tp2"), self.sem("h2")

            order = self.tile_order(sample_first=True)
            npt = self.NPT // TT
            dst = self.y_out if post == "final" else self.xres

            def post_pe(g):
                PE.wait_ge(s_xsb2.h, s_xsb2.base + g + 1)
                if g >= 1:
                    PE.wait_ge(s_h2.h, s_h2.base + g)
                for c in range(NKC):
                    ins = PE.transpose(out=tpB[:, c * P:(c + 1) * P], in_=xsb2[g % 2][:, c * P:(c + 1) * P], identity=self.ident[:])
                s_tp2.inc(ins)

            for ti, t in enumerate(order):
                t0 = t * TT
                for s in range(4):
                    g = ti * 4 + s
                    r0 = t0 + s * P
                    if g >= 2:
                        SP.wait_ge(s_xsb.h, s_xsb.base + g - 1)
                    ld_x[g % 2].inc(SP.dma_start(out=xin[g % 2][:], in_=src[r0:r0 + P, :]), 16)
                    ACT.wait_ge(ld_x[g % 2].h, ld_x[g % 2].base + 16 * (g // 2 + 1))
                    if g >= 4:
                        ACT.wait_ge(s_tp.h, s_tp.base + g - 3)
                    self.chain(ACT, ACT.activation(out=xsb[s][:], in_=xin[g % 2][:], func=AF.Square, accum_out=ss[:, s:s + 1]))
                    s_ss.inc(ACT.activation(out=ss[:, 4 + s:5 + s], in_=ss[:, s:s + 1], func=AF.Sqrt, scale=1.0 / D, bias=self.epsc[:, 0:1]))
                    DVE.wait_ge(s_ss.h, s_ss.base + g + 1)
                    self.chain(DVE, DVE.reciprocal(out=ss[:, 4 + s:5 + s], in_=ss[:, 4 + s:5 + s]))
                    s_xsb.inc(DVE.scalar_tensor_tensor(out=xsb[s][:], in0=xin[g % 2][:], scalar=ss[:, 4 + s:5 + s], in1=gb1[:],
                                                       op0=ALU.mult, op1=ALU.mult))
                    PE.wait_ge(s_xsb.h, s_xsb.base + g + 1)
                    if g >= 1:
                        PE.wait_ge(s_ht.h, s_ht.base + g)
                    for c in range(NKC):
                        ins = PE.transpose(out=tpA[:, c * P:(c + 1) * P], in_=xsb[s][:, c * P:(c + 1) * P], identity=self.ident[:])
                    s_tp.inc(ins)
                    ACT.wait_ge(s_tp.h, s_tp.base + g + 1)
                    s_ht.inc(ACT.activation(out=hT[:, :, s * P:(s + 1) * P], in_=tpA[:, :].rearrange("p (c t) -> p c t", c=NKC), func=AF.Copy))
                PE.wait_ge(s_ht.h, s_ht.base + 4 * (ti + 1))
                for j in range(NJ):
                    J = ti * NJ + j
                    if J >= 2:
                        PE.wait_ge(s_act.h, s_act.base + J - 1)
                    for k in range(NKC):
                        PE.matmul(G[J % 2][:], win[:, k, j * P:(j + 1) * P], hT[:, k, :], start=(k == 0), stop=(k == NKC - 1))
                    for k in range(NKC):
                        ins = PE.matmul(U[J % 2][:], win[:, k, DFF + j * P:DFF + (j + 1) * P], hT[:, k, :], start=(k == 0), stop=(k == NKC - 1))
                    s_gu.inc(ins)
                    ACT.wait_ge(s_gu.h, s_gu.base + J + 1)
                    if J >= 1:
                        ACT.wait_ge(s_act.h, s_act.base + J)
                    s_sg.inc(ACT.activation(out=sg[:], in_=G[J % 2][:], func=AF.Silu))
                    DVE.wait_ge(s_sg.h, s_sg.base + J + 1)
                    s_act.inc(DVE.tensor_tensor(out=actT[:, j, :], in0=sg[:], in1=U[J % 2][:], op=ALU.mult))
                PE.wait_ge(s_act.h, s_act.base + NJ * (ti + 1))
                for s in range(4):
                    g = ti * 4 + s
                    r0 = t0 + s * P
                    slot = g % 2
                    if g >= 2:
                        SP.wait_ge(st_x[slot].h, st_x[slot].base + 16 * (g // 2))
                        if post == "mix":
                            SP.wait_ge(s_xsb2.h, s_xsb2.base + g - 1)
                    ld_xr[slot].inc(SP.dma_start(out=xr[slot][:], in_=src[r0:r0 + P, :]), 16)
                    for half in range(2):
                        g2 = g * 2 + half
                        if g2 >= 2:
                            PE.wait_ge(s_res.h, s_res.base + g2 - 1)
                        for j in range(NJ):
                            ins = PE.matmul(Y[g2 % 2][:], actT[:, j, s * P:(s + 1) * P], wout[:, j, half * TT:(half + 1) * TT],
                                            start=(j == 0), stop=(j == NJ - 1))
                        s_y.inc(ins)
                        DVE.wait_ge(s_y.h, s_y.base + g2 + 1)
                        if half == 0:
                            DVE.wait_ge(ld_xr[slot].h, ld_xr[slot].base + 16 * (g // 2 + 1))
                        s_res.inc(DVE.scalar_tensor_tensor(out=xr[slot][:, half * TT:(half + 1) * TT], in0=Y[g2 % 2][:], scalar=0.5,
                                                           in1=xr[slot][:, half * TT:(half + 1) * TT], op0=ALU.mult, op1=ALU.add))
                    if post == "mix" and s >= 1:
                        post_pe(g - 1)
                    if post != "final":
                        POOL.wait_ge(s_res.h, s_res.base + 2 * g + 2)
                        ins = POOL.dma_start(out=dst[r0:r0 + P, :], in_=xr[slot][:])
                        st_x[slot].inc(ins, 16)
                    if post == "mix":
                        ACT.wait_ge(s_res.h, s_res.base + 2 * g + 2)
                        if g >= 2:
                            ACT.wait_ge(s_tp2.h, s_tp2.base + g - 1)
                        self.chain(ACT, ACT.activation(out=xsb2[slot][:], in_=xr[slot][:], func=AF.Square, accum_out=ss[:, 8 + slot:9 + slot]))
                        s_ss2.inc(ACT.activation(out=ss[:, 12 + slot:13 + slot], in_=ss[:, 8 + slot:9 + slot], func=AF.Sqrt, scale=1.0 / D, bias=self.epsc[:, 0:1]))
                        DVE.wait_ge(s_ss2.h, s_ss2.base + g + 1)
                        self.chain(DVE, DVE.reciprocal(out=ss[:, 12 + slot:13 + slot], in_=ss[:, 12 + slot:13 + slot]))
                        s_xsb2.inc(DVE.scalar_tensor_tensor(out=xsb2[slot][:], in0=xr[slot][:], scalar=ss[:, 12 + slot:13 + slot], in1=gb2[:],
                                                            op0=ALU.mult, op1=ALU.mult))
                        if s == 3:
                            post_pe(g)
                        for gg in ([g - 1] if 1 <= s < 3 else ([g - 1, g] if s == 3 else [])):
                            sl = gg % 2
                            ACT.wait_ge(s_tp2.h, s_tp2.base + gg + 1)
                            if gg >= 2:
                                ACT.wait_ge(st_h2[sl].h, st_h2[sl].base + 16 * (gg // 2))
                            s_h2.inc(ACT.activation(out=h2st[sl][:], in_=tpB[:, :].rearrange("p (c t) -> p c t", c=NKC), func=AF.Copy))
                            POOL.wait_ge(s_h2.h, s_h2.base + gg + 1)
                            tg = order[gg // 4]
                            rr = tg * TT + (gg % 4) * P
                            if tg < npt:
                                hd = self.hp.rearrange("(c p) t -> p c t", p=P)[:, :, rr:rr + P]
                            else:
                                rr -= self.NPT
                                hd = self.hs_in[rr // TT].rearrange("(c p) t -> p c t", p=P)[:, :, rr % TT:rr % TT + P]
                            ins = POOL.dma_start(out=hd, in_=h2st[sl][:])
                            st_h2[sl].inc(ins, 16)
                    elif post == "final":
                        ACT.wait_ge(s_res.h, s_res.base + 2 * g + 2)
                        self.chain(ACT, ACT.activation(out=xsb[s][:], in_=xr[slot][:], func=AF.Square, accum_out=ss[:, 8 + slot:9 + slot]))
                        s_ss2.inc(ACT.activation(out=ss[:, 12 + slot:13 + slot], in_=ss[:, 8 + slot:9 + slot], func=AF.Sqrt, scale=1.0 / D, bias=self.epsc[:, 0:1]))
                        DVE.wait_ge(s_ss2.h, s_ss2.base + g + 1)
                        self.chain(DVE, DVE.reciprocal(out=ss[:, 12 + slot:13 + slot], in_=ss[:, 12 + slot:13 + slot]))
                        s_xsb2.inc(DVE.scalar_tensor_tensor(out=xr[slot][:], in0=xr[slot][:], scalar=ss[:, 12 + slot:13 + slot], in1=gb2[:],
                                                            op0=ALU.mult, op1=ALU.mult))
                        POOL.wait_ge(s_xsb2.h, s_xsb2.base + g + 1)
                        st_x[slot].inc(POOL.dma_start(out=dst[r0:r0 + P, :], in_=xr[slot][:]), 16)
                if post == "mix" and ti < (self.SC // TT):
                    for sl in range(2):
                        POOL.wait_ge(st_h2[sl].h, st_h2[sl].v)
                    kk = t - npt
                    self.cc.inc(POOL.collective_compute("AllGather", ALU.bypass, replica_groups=[[0, 1, 2, 3], [4, 5, 6, 7]],
                                                        ins=[self.hs_in[kk][:, :]], outs=[self.hs_all[kk][:, :]]))
            for sl in range(2):
                for e in (SP, POOL, ACT, DVE, PE):
                    e.wait_ge(st_x[sl].h, st_x[sl].v)
                    if post == "mix":
                        e.wait_ge(st_h2[sl].h, st_h2[sl].v)

    def add_phase(self):
        nc = self.nc
        self.phase_begin()
        SP, DVE, POOL = nc.sync, nc.vector, nc.gpsimd
        with ExitStack() as es:
            a = [es.enter_context(self.sbt(f"adda{i}", [P, D], F32)) for i in range(2)]
            b = [es.enter_context(self.sbt(f"addb{i}", [P, D], F32)) for i in range(2)]
            ld = [self.sem("addld0"), self.sem("addld1")]
            st = [self.sem("addst0"), self.sem("addst1")]
            s_add = self.sem("add")
            n = self.SC // P
            for g in range(n):
                sl = g % 2
                if g >= 2:
                    SP.wait_ge(st[sl].h, st[sl].base + 16 * (g // 2))
                ld[sl].inc(SP.dma_start(out=a[sl][:], in_=self.xres[self.NPT + g * P:self.NPT + (g + 1) * P, :]), 16)
                ld[sl].inc(SP.dma_start(out=b[sl][:], in_=self.ys_out[(g * P) // TT][(g * P) % TT:(g * P) % TT + P, :]), 16)
                DVE.wait_ge(ld[sl].h, ld[sl].base + 32 * (g // 2 + 1))
                s_add.inc(DVE.tensor_tensor(out=a[sl][:], in0=a[sl][:], in1=b[sl][:], op=ALU.add))
                POOL.wait_ge(s_add.h, s_add.base + g + 1)
                st[sl].inc(POOL.dma_start(out=self.xres[self.NPT + g * P:self.NPT + (g + 1) * P, :], in_=a[sl][:]), 16)
            for sl in range(2):
                for e in (SP, POOL, DVE, nc.scalar, nc.tensor):
                    e.wait_ge(st[sl].h, st[sl].v)

    def qkv_phase(self, kind, j):
        nc = self.nc
        self.phase_begin()
        PE, ACT, DVE, POOL, SP = nc.tensor, nc.scalar, nc.vector, nc.gpsimd, nc.sync
        if kind == "A":
            wp, ws = self.a_w_qkv[j], self.a_w_qkv_s[j]
            ncp, ncs = 3072, 768
            qk_p = [(h * P, 0.125) for h in range(8)] + [(1024 + h * P, 1.0) for h in range(8)]
            qk_s = [(h * P, 0.125) for h in range(2)] + [(256 + h * P, 1.0) for h in range(2)]
            v_p, v_s = (2048, 1024), (512, 256)
        elif kind == "B":
            wp, ws = self.b_w_qkv, self.b_w_qkv_s
            ncp, ncs = 2048, 512
            qk_p = [(c * P, "q") for c in range(8)] + [(1024 + g * P, "k") for g in range(4)]
            qk_s = [(c * P, "q") for c in range(2)] + [(256, "k")]
            v_p, v_s = (1536, 512), (384, 128)
        else:
            wp, ws = self.c_w_qkv, self.c_w_qkv_s
            ncp, ncs = 3072, 768
            qk_p = [(c * P, 0.125) for c in range(8)] + [(1024 + c * P, 1.0) for c in range(8)]
            qk_s = [(c * P, 0.125) for c in range(2)] + [(256 + c * P, 1.0) for c in range(2)]
            v_p, v_s = (2048, 1024), (512, 256)
        with ExitStack() as es:
            sb = lambda n, s, d: es.enter_context(self.sbt(n, s, d))
            pb = lambda n, s, d: es.enter_context(self.pst(n, s, d))
            wpt = sb("wqp", [P, NKC, ncp], BF16)
            wst = sb("wqs", [P, NKC, ncs], BF16)
            hT = [sb(f"qhT{i}", [P, NKC, TT], BF16) for i in range(2)]
            qkst = [sb(f"qkst{i}", [P, 16, TT], BF16) for i in range(2)]
            vst = [sb(f"vst{i}", [P, 4, 1024], BF16) for i in range(2)]
            acc = [pb(f"qacc{i}", [P, TT], F32) for i in range(2)]
            vacc = [pb(f"vacc{i}", [P, TT], F32) for i in range(2)]
            if kind == "B":
                gn = sb("bgn", [P, 2], F32)
                blk = sb("bblk", [P, P], BF16)
                swp = sb("bswp", [P, P], BF16)
                rope = [sb(f"rope{i}", [P, 2, TT], F32) for i in range(2)]
                sqb = sb("bsq", [P, TT], BF16)
                rr = sb("brr", [P, TT], F32)
                qnb = sb("bqn", [P, TT], BF16)
                t1 = sb("bt1", [P, TT], F32)
                t2 = sb("bt2", [P, TT], F32)
                msq = pb("bmsq", [P, TT], F32)
                rot = pb("brot", [P, TT], F32)
            wl = self.sem("qwl")
            for k in range(NKC):
                wl.inc(POOL.dma_start(out=wpt[:, k, :], in_=wp.rearrange("(k p) n -> p k n", p=P)[:, k, :]), 16)
            wl.inc(POOL.dma_start(out=wst[:], in_=ws.rearrange("(k p) n -> p k n", p=P)), 16)
            if kind == "B":
                wl.inc(POOL.dma_start(out=blk[:], in_=self.k_blk64[:, :]), 16)
                wl.inc(POOL.dma_start(out=swp[:], in_=self.k_swap[:, :]), 16)
                wlh = self.sem("wlh")
                wlh.inc(SP.dma_start(out=gn[:, 0:1], in_=self.b_qk_norm[0:1, :].rearrange("a p -> p a")), 16)
                wlh.inc(SP.dma_start(out=gn[:, 1:2], in_=self.b_qk_norm[1:2, :].rearrange("a p -> p a")), 16)
                DVE.wait_ge(wlh.h, wlh.v)
            for e in (PE, DVE):
                e.wait_ge(wl.h, wl.v)
            ld_h = [self.sem("qldh0"), self.sem("qldh1")]
            ld_r = [self.sem("qldr0"), self.sem("qldr1")]
            st_q = [self.sem("qstq0"), self.sem("qstq1")]
            st_v = [self.sem("qstv0"), self.sem("qstv1")]
            s_mm, s_ev, s_vmm, s_vev, s_done = self.sem("qmm"), self.sem("qev"), self.sem("qvmm"), self.sem("qvev"), self.sem("qdone")
            s_b = [self.sem(f"qb{i}") for i in range(6)]
            npt = self.NPT // TT
            nst = self.SS // TT
            nsc = self.SC // TT
            tiles = [("s", i) for i in range(nst)] + [("p", i) for i in range(npt)]
            vend = []
            mend = []
            stq_after, stv_after = [], []
            M = 0
            V = 0
            for ti, (grp, i) in enumerate(tiles):
                sl = ti % 2
                if grp == "p":
                    hsrc = self.hp.rearrange("(c p) t -> p c t", p=P)[:, :, i * TT:(i + 1) * TT]
                    w, qk, (vc0, vw) = wpt, qk_p, v_p
                    qdst = self.qkT_p[0:len(qk_p)].rearrange("m p t -> p m t")[:, :, i * TT:(i + 1) * TT]
                    vdst = self.v_p[i * TT:(i + 1) * TT, 0:vw].rearrange("(s p) e -> p s e", p=P)
                    pos0 = (i * TT) % self.SP
                else:
                    r = i // nsc
                    hsrc = self.hs_all[i % nsc].rearrange("(r c p) t -> p r c t", p=P, c=NKC)[:, r, :, :]
                    w, qk, (vc0, vw) = wst, qk_s, v_s
                    qdst = self.qkT_s[0:len(qk_s)].rearrange("m p t -> p m t")[:, :, i * TT:(i + 1) * TT]
                    vdst = self.v_s[i * TT:(i + 1) * TT, 0:vw].rearrange("(s p) e -> p s e", p=P)
                    pos0 = i * TT
                if ti >= 2:
                    SP.wait_ge(s_vmm.h, s_vmm.base + vend[ti - 2])
                ld_h[sl].inc(SP.dma_start(out=hT[sl][:], in_=hsrc), 16)
                if kind == "B":
                    if ti >= 2:
                        SP.wait_ge(s_ev.h, s_ev.base + mend[ti - 2])
                    ld_r[sl].inc(SP.dma_start(out=rope[sl][:], in_=self.k_rope.rearrange("a p t -> p a t")[:, :, pos0:pos0 + TT]), 16)
                    DVE.wait_ge(ld_r[sl].h, ld_r[sl].base + 16 * (ti // 2 + 1))
                PE.wait_ge(ld_h[sl].h, ld_h[sl].base + 16 * (ti // 2 + 1))
                if ti >= 2:
                    for e in (ACT, DVE):
                        e.wait_ge(st_q[sl].h, stq_after[ti - 2])
                        e.wait_ge(st_v[sl].h, stv_after[ti - 2])
                for m, (c0, mode) in enumerate(qk):
                    a = M % 2
                    if M >= 2:
                        PE.wait_ge(s_ev.h, s_ev.base + M - 1)
                    for k in range(NKC):
                        ins = PE.matmul(acc[a][:], w[:, k, c0:c0 + P], hT[sl][:, k, :], start=(k == 0), stop=(k == NKC - 1))
                    s_mm.inc(ins)
                    if kind != "B":
                        ACT.wait_ge(s_mm.h, s_mm.base + M + 1)
                        s_ev.inc(ACT.activation(out=qkst[sl][:, m, :], in_=acc[a][:], func=AF.Copy, scale=float(mode)))
                    else:
                        gi = 0 if mode == "q" else 1
                        ACT.wait_ge(s_mm.h, s_mm.base + M + 1)
                        if M >= 1:
                            ACT.wait_ge(s_b[1].h, s_b[1].base + M)
                        s_b[0].inc(ACT.activation(out=sqb[:], in_=acc[a][:], func=AF.Square))
                        PE.wait_ge(s_b[0].h, s_b[0].base + M + 1)
                        if M >= 1:
                            PE.wait_ge(s_b[2].h, s_b[2].base + M)
                        s_b[1].inc(PE.matmul(msq[:], blk[:], sqb[:], start=True, stop=True))
                        ACT.wait_ge(s_b[1].h, s_b[1].base + M + 1)
                        if M >= 1:
                            ACT.wait_ge(s_b[2].h, s_b[2].base + M)
                        s_b[3].inc(ACT.activation(out=rr[:], in_=msq[:], func=AF.Sqrt, bias=self.epsc[:, 0:1]))
                        DVE.wait_ge(s_b[3].h, s_b[3].base + M + 1)
                        self.chain(DVE, DVE.reciprocal(out=rr[:], in_=rr[:]))
                        if M >= 1:
                            DVE.wait_ge(s_b[4].h, s_b[4].base + M)
                        s_b[2].inc(DVE.scalar_tensor_tensor(out=qnb[:], in0=acc[a][:], scalar=gn[:, gi:gi + 1], in1=rr[:],
                                                            op0=ALU.mult, op1=ALU.mult))
                        PE.wait_ge(s_b[2].h, s_b[2].base + M + 1)
                        if M >= 1:
                            PE.wait_ge(s_ev.h, s_ev.base + M)
                        s_b[4].inc(PE.matmul(rot[:], swp[:], qnb[:], start=True, stop=True))
                        DVE.wait_ge(s_b[2].h, s_b[2].base + M + 1)
                        DVE.tensor_tensor(out=t1[:], in0=qnb[:], in1=rope[sl][:, 0, :], op=ALU.mult)
                        DVE.wait_ge(s_b[4].h, s_b[4].base + M + 1)
                        self.chain(DVE, DVE.tensor_tensor(out=t2[:], in0=rot[:], in1=rope[sl][:, 1, :], op=ALU.mult))
                        s_ev.inc(DVE.tensor_tensor(out=qkst[sl][:, m, :], in0=t1[:], in1=t2[:], op=ALU.add))
                    M += 1
                mend.append(M)
                POOL.wait_ge(s_ev.h, s_ev.base + M)
                for m0 in range(0, len(qk), 8):
                    m1 = min(len(qk), m0 + 8)
                    st_q[sl].inc(POOL.dma_start(out=qdst[:, m0:m1, :], in_=qkst[sl][:, m0:m1, :]), 16)
                stq_after.append(st_q[sl].v)
                nh = (vw + TT - 1) // TT
                for s in range(4):
                    for hh in range(nh):
                        a = V % 2
                        wv = min(TT, vw - hh * TT)
                        if V >= 2:
                            PE.wait_ge(s_vev.h, s_vev.base + V - 1)
                        for k in range(NKC):
                            ins = PE.matmul(vacc[a][:, 0:wv], hT[sl][:, k, s * P:(s + 1) * P], w[:, k, vc0 + hh * TT:vc0 + hh * TT + wv],
                                            start=(k == 0), stop=(k == NKC - 1))
                        s_vmm.inc(ins)
                        ACT.wait_ge(s_vmm.h, s_vmm.base + V + 1)
                        s_vev.inc(ACT.activation(out=vst[sl][:, s, hh * TT:hh * TT + wv], in_=vacc[a][:, 0:wv], func=AF.Copy))
                        V += 1
                POOL.wait_ge(s_vev.h, s_vev.base + V)
                st_v[sl].inc(POOL.dma_start(out=vdst, in_=vst[sl][:, :, 0:vw]), 16)
                stv_after.append(st_v[sl].v)
                vend.append(V)
            for sl in range(2):
                for e in (SP, POOL, ACT, DVE, PE):
                    e.wait_ge(st_q[sl].h, st_q[sl].v)
                    e.wait_ge(st_v[sl].h, st_v[sl].v)

    def attn_phase(self, kind, j, li):
        nc = self.nc
        self.phase_begin()
        PE, ACT, DVE, POOL, SP = nc.tensor, nc.scalar, nc.vector, nc.gpsimd, nc.sync
        SPq, SSq = self.SP, self.SS
        QRP, QRS = min(2048, SPq), min(8192, SSq)
        lam_init = 0.8 - 0.6 * math.exp(-0.3 * li)
        if kind == "A":
            wo_p, wo_s = self.a_w_out[j], self.a_w_out_s[j]
        elif kind == "B":
            wo_p, wo_s = self.b_w_out, self.b_w_out_s
        else:
            wo_p, wo_s = self.c_w_out, self.c_w_out_s
        with ExitStack() as es:
            sb = lambda n, s, d: es.enter_context(self.sbt(n, s, d))
            pb = lambda n, s, d: es.enter_context(self.pst(n, s, d))
            On = sb("On", [P, max(8 * QRP, 2 * QRS)], BF16)
            kt = sb("kt", [P, max(SPq, SSq)], BF16)
            vt = sb("vt", [P, max(SPq, SSq) // P, P], BF16)
            qtl = [sb(f"qtl{i}", [P, TT], BF16) for i in range(2)]
            pt = [[sb(f"pt{a}{i}", [P, TT], BF16) for i in range(2)] for a in range(2)]
            tmp = [sb(f"tmp{i}", [P, TT], F32) for i in range(2)]
            fr = [sb(f"fr{i}", [P, TT], F32) for i in range(3)]
            sqf = sb("sqf", [P, TT], BF16)
            Lacc = [sb(f"Lacc{i}", [P, TT], F32) for i in range(2)]
            wop = sb("wop", [P, 8, D], BF16)
            wos = sb("wos", [P, 2, D], BF16)
            xr = [sb(f"axr{i}", [P, D], F32) for i in range(2)]
            S = [[pb(f"S{a}{i}", [P, TT], F32) for i in range(2)] for a in range(2)]
            O = [pb(f"O{i}", [P, TT], F32) for i in range(2)]
            L = [pb(f"L{i}", [P, TT], F32) for i in range(2)]
            wl = self.sem("awl")
            wl.inc(POOL.dma_start(out=wop[:], in_=wo_p.rearrange("(c p) n -> p c n", p=P)), 16)
            wl.inc(POOL.dma_start(out=wos[:], in_=wo_s.rearrange("(c p) n -> p c n", p=P)), 16)
            if kind == "A":
                t5t = sb("t5t", [P, 3, P], F32)
                cb = sb("cb", [P, 2, 10], F32)
                lam = sb("lam", [P, 256], F32)
                lsc = sb("lsc", [P, 8], F32)
                gsub = sb("gsub", [P, 1], F32)
                oh = sb("aoh", [32, 256], F32)
                tb = sb("atb", [32, 10], F32)
                wlh = self.sem("wlh")
                wlh.inc(SP.dma_start(out=oh[:], in_=self.k_t5oh[:, 511:767]), 16)
                wlh.inc(SP.dma_start(out=tb[:], in_=self.t5_all[:, :]), 16)
                wlh.inc(SP.dma_start(out=lam[:], in_=self.a_lambda[j:j + 1, :].partition_broadcast(P)), 16)
                wlh.inc(SP.dma_start(out=gsub[:], in_=self.a_subln[j:j + 1, :].rearrange("o p -> p o")), 16)
                for e in (PE, DVE, ACT):
                    e.wait_ge(wlh.h, wlh.v)
            if kind == "C":
                ntp, nts = len(self.na_table_p), len(self.na_table_s)
                nab = sb("nab", [P, 2, max(ntp, nts), TT], BF16)
            for e in (PE, DVE, ACT):
                e.wait_ge(wl.h, wl.v)
            s_pre = self.sem("apre")
            if kind == "A":
                for sg_ in range(2):
                    PE.matmul(S[0][sg_][:, 0:10], oh[:, sg_ * P:(sg_ + 1) * P], tb[:, :], start=True, stop=True)
                s_pre.inc(PE.matmul(S[1][0][:, 0:10], oh[:, 0:P], tb[:, :], start=True, stop=True))
                DVE.wait_ge(s_pre.h, s_pre.base + 1)
                DVE.tensor_copy(out=cb[:, 0, :], in_=S[0][0][:, 0:10])
                DVE.tensor_copy(out=cb[:, 1, :], in_=S[0][1][:, 0:10])
                DVE.tensor_tensor(out=lam[:, 0:64], in0=lam[:, 0:64], in1=lam[:, 64:128], op=ALU.mult)
                self.chain(DVE, DVE.tensor_tensor(out=lam[:, 128:192], in0=lam[:, 128:192], in1=lam[:, 192:256], op=ALU.mult))
                DVE.reduce_sum(out=lsc[:, 0:1], in_=lam[:, 0:64], axis=mybir.AxisListType.X)
                s_pre.inc(DVE.reduce_sum(out=lsc[:, 1:2], in_=lam[:, 128:192], axis=mybir.AxisListType.X))
                ACT.wait_ge(s_pre.h, s_pre.base + 2)
                s_pre.inc(ACT.activation(out=lsc[:, 2:4], in_=lsc[:, 0:2], func=AF.Exp))
                DVE.wait_ge(s_pre.h, s_pre.base + 3)
                self.chain(DVE, DVE.tensor_tensor(out=lsc[:, 4:5], in0=lsc[:, 3:4], in1=lsc[:, 2:3], op=ALU.subtract))
                DVE.tensor_scalar(out=lsc[:, 5:6], in0=lsc[:, 4:5], scalar1=-lam_init, scalar2=None, op0=ALU.add)
                s_pre.inc(DVE.tensor_scalar(out=gsub[:], in0=gsub[:], scalar1=1.0 - lam_init, scalar2=None, op0=ALU.mult))
                for e in (PE, ACT, DVE):
                    e.wait_ge(s_pre.h, s_pre.base + 4)
                neg_lam = lsc[:, 5:6]

            ld_kv, ld_b = self.sem("ldkv"), self.sem("ldb")
            ld_q = [self.sem("ldq0"), self.sem("ldq1")]
            ld_xr = [self.sem("aldx0"), self.sem("aldx1")]
            st_o = [self.sem("asto0"), self.sem("asto1")]
            s_qk, s_tmp, s_p, s_pv = self.sem("qk"), self.sem("tmp"), self.sem("p"), self.sem("pv")
            s_f1, s_f2, s_f3, s_f4, s_fin = self.sem("f1"), self.sem("f2"), self.sem("f3"), self.sem("f4"), self.sem("fin")
            s_ym, s_ye = self.sem("ym"), self.sem("ye")
            s_l = [self.sem("lacc0"), self.sem("lacc1")]
            s_lm = self.sem("lmm")
            cnt = dict(I=0, QT=0, U=0, NT=0, FQ=0, Y=0, G=0)
            qk_end = []
            pv_end_unit = [0]

            jobs = []
            for q0 in range(0, SSq, QRS):
                jobs.append(("s", 0, q0, QRS))
            for sq in range(self.NPC):
                for q0 in range(0, SPq, QRP):
                    jobs.append(("p", sq, q0, QRP))

            for (grp, sq, q0, QR) in jobs:
                Sk = SSq if grp == "s" else SPq
                nu = 2 if grp == "s" else 8
                tok0 = 0 if grp == "s" else sq * SPq
                qkT = self.qkT_s if grp == "s" else self.qkT_p
                vsrc = self.v_s if grp == "s" else self.v_p
                nkc = Sk // P
                Onv = On[:, 0:nu * QR].rearrange("p (u t) -> p u t", u=nu)
                for u in range(nu):
                    if kind == "B":
                        kidx = (2 if grp == "s" else 8) + u // 2
                        vcol = (u // 2) * P if grp == "p" else 0
                    else:
                        kidx = (2 if grp == "s" else 8) + u
                        vcol = u * P
                    Uc = cnt["U"]
                    SP.wait_ge(s_pv.h, s_pv.base + pv_end_unit[-1])
                    SP.wait_ge(s_fin.h, s_fin.base + cnt["FQ"])
                    ld_kv.inc(SP.dma_start(out=kt[:, 0:Sk], in_=qkT[kidx, :, tok0:tok0 + Sk]), 16)
                    vview = vsrc[tok0:tok0 + Sk, vcol:vcol + P].rearrange("(c p) e -> p c e", p=P)
                    for c0_ in range(0, nkc, 16):
                        c1_ = min(nkc, c0_ + 16)
                        ld_kv.inc(SP.dma_start(out=vt[:, c0_:c1_, :], in_=vview[:, c0_:c1_, :]), 16)
                    PE.wait_ge(ld_kv.h, ld_kv.v)
                    if kind == "A":
                        hcol = (8 + u) if grp == "s" else u
                        ld_b.inc(SP.dma_start(out=t5t[:], in_=self.t5tiles[:, hcol].rearrange("d k q -> k d q")), 16)
                        DVE.wait_ge(ld_b.h, ld_b.v)
                    if kind == "C":
                        nbs = self.na_bias_s if grp == "s" else self.na_bias_p
                        ntl = nts if grp == "s" else ntp
                        for hh_ in range(2):
                            for n0_ in range(0, ntl, 4):
                                n1_ = min(ntl, n0_ + 4)
                                ld_b.inc(SP.dma_start(out=nab[:, hh_, n0_:n1_, :], in_=nbs[2 * u + hh_, n0_:n1_].rearrange("n k q -> k n q")), 16)
                        DVE.wait_ge(ld_b.h, ld_b.v)
                    cnt["U"] += 1
                    for qtl_i in range(QR // TT):
                        qpos = q0 + qtl_i * TT
                        tq = qpos // TT
                        QT = cnt["QT"]
                        qs = QT % 2
                        if QT >= 2:
                            SP.wait_ge(s_qk.h, s_qk.base + qk_end[QT - 2])
                        ld_q[qs].inc(SP.dma_start(out=qtl[qs][:], in_=qkT[u, :, tok0 + qpos:tok0 + qpos + TT]), 16)
                        PE.wait_ge(ld_q[qs].h, ld_q[qs].base + 16 * (QT // 2 + 1))
                        if kind == "C":
                            plan = (self.na_plan_s if grp == "s" else self.na_plan_p)[tq]
                            chunks = [(kc, "tile", tid) for (kc, tid) in plan]
                        elif kind == "B":
                            chunks = [(kc, "plain", None) for kc in range(nkc)]
                        else:
                            chunks = []
                            for kc in range(nkc):
                                ds_ = [kc - (4 * tq + b) for b in range(4)]
                                if all(d <= -2 for d in ds_):
                                    chunks.append((kc, "far", 0))
                                elif all(d >= 2 for d in ds_):
                                    chunks.append((kc, "far", 1))
                                else:
                                    chunks.append((kc, "near", ds_))
                        PE.wait_ge(s_fin.h, s_fin.base + cnt["FQ"])
                        n = len(chunks)

                        def emit_qk(ci):
                            I = cnt["I"] + ci
                            kc = chunks[ci][0]
                            a = I % 2
                            if I >= 2:
                                PE.wait_ge(s_p.h, s_p.base + I - 1)
                            PE.matmul(S[a][0][:], kt[0:64, kc * P:(kc + 1) * P], qtl[qs][0:64, :], start=True, stop=True)
                            s_qk.inc(PE.matmul(S[a][1][:], kt[64:128, kc * P:(kc + 1) * P], qtl[qs][64:128, :], start=True, stop=True))

                        emit_qk(0)
                        for ci in range(n):
                            I = cnt["I"] + ci
                            a = I % 2
                            kc, mode, info = chunks[ci]
                            if ci + 1 < n:
                                emit_qk(ci + 1)
                            if I >= 2:
                                ACT.wait_ge(s_pv.h, s_pv.base + I - 1)
                                ACT.wait_ge(s_l[0].h, s_l[0].base + I - 1)
                                ACT.wait_ge(s_l[1].h, s_l[1].base + I - 1)
                            if mode in ("plain", "far"):
                                ACT.wait_ge(s_qk.h, s_qk.base + I + 1)
                                for i in range(2):
                                    if mode == "plain":
                                        ins = ACT.activation(out=pt[a][i][:], in_=S[a][i][:], func=AF.Exp, scale=0.125)
                                    else:
                                        ins = ACT.activation(out=pt[a][i][:], in_=S[a][i][:], func=AF.Exp, bias=cb[:, info, hcol:hcol + 1])
                                s_p.inc(ins)
                            else:
                                NT = cnt["NT"]
                                DVE.wait_ge(s_qk.h, s_qk.base + I + 1)
                                if NT >= 1:
                                    DVE.wait_ge(s_p.h, s_p.base + cnt["lastnear"] + 1)
                                for i in range(2):
                                    if mode == "tile":
                                        ins = DVE.tensor_tensor(out=tmp[i][:], in0=S[a][i][:], in1=nab[:, i, info, :], op=ALU.add)
                                    else:
                                        for b in range(4):
                                            d = info[b]
                                            cs = slice(b * P, (b + 1) * P)
                                            if abs(d) <= 1:
                                                ins = DVE.tensor_tensor(out=tmp[i][:, cs], in0=S[a][i][:, cs], in1=t5t[:, d + 1, :], op=ALU.add)
                                            else:
                                                sg_ = 0 if d < 0 else 1
                                                ins = DVE.tensor_scalar(out=tmp[i][:, cs], in0=S[a][i][:, cs], scalar1=cb[:, sg_, hcol:hcol + 1],
                                                                        scalar2=None, op0=ALU.add)
                                s_tmp.inc(ins)
                                cnt["NT"] += 1
                                cnt["lastnear"] = I
                                ACT.wait_ge(s_tmp.h, s_tmp.base + cnt["NT"])
                                for i in range(2):
                                    ins = ACT.activation(out=pt[a][i][:], in_=tmp[i][:], func=AF.Exp)
                                s_p.inc(ins)
                            PE.wait_ge(s_p.h, s_p.base + I + 1)
                            for i in range(2):
                                ins = PE.matmul(O[i][:], vt[:, kc, :], pt[a][i][:], start=(ci == 0), stop=(ci == n - 1))
                            s_pv.inc(ins)
                            for i, eng in ((0, DVE), (1, POOL)):
                                eng.wait_ge(s_p.h, s_p.base + I + 1)
                                if ci == 0:
                                    s_l[i].inc(eng.tensor_copy(out=Lacc[i][:], in_=pt[a][i][:]))
                                else:
                                    eng.wait_ge(s_l[i].h, s_l[i].base + I)
                                    s_l[i].inc(eng.tensor_tensor(out=Lacc[i][:], in0=Lacc[i][:], in1=pt[a][i][:], op=ALU.add))
                        cnt["I"] += n
                        qk_end.append(cnt["I"])
                        cnt["QT"] += 1
                        Iend = cnt["I"]
                        FQ = cnt["FQ"]
                        oc = slice(qtl_i * TT, (qtl_i + 1) * TT)
                        for i in range(2):
                            PE.wait_ge(s_l[i].h, s_l[i].base + Iend)
                            ins = PE.matmul(L[i][:], self.ones_f[:], Lacc[i][:], start=True, stop=True)
                        s_lm.inc(ins)
                        DVE.wait_ge(s_pv.h, s_pv.base + Iend)
                        DVE.wait_ge(s_lm.h, s_lm.base + FQ + 1)
                        if kind == "A":
                            DVE.reciprocal(out=fr[0][:], in_=L[0][:])
                            self.chain(DVE, DVE.reciprocal(out=fr[1][:], in_=L[1][:]))
                            DVE.tensor_tensor(out=fr[0][:], in0=O[0][:], in1=fr[0][:], op=ALU.mult)
                            self.chain(DVE, DVE.tensor_tensor(out=fr[1][:], in0=O[1][:], in1=fr[1][:], op=ALU.mult))
                            s_f1.inc(DVE.scalar_tensor_tensor(out=fr[2][:], in0=fr[1][:], scalar=neg_lam, in1=fr[0][:], op0=ALU.mult, op1=ALU.add))
                            ACT.wait_ge(s_f1.h, s_f1.base + FQ + 1)
                            s_f2.inc(ACT.activation(out=sqf[:], in_=fr[2][:], func=AF.Square))
                            PE.wait_ge(s_f2.h, s_f2.base + FQ + 1)
                            s_f3.inc(PE.matmul(S[0][0][:], self.onesm[:], sqf[:], start=True, stop=True))
                            ACT.wait_ge(s_f3.h, s_f3.base + FQ + 1)
                            s_f4.inc(ACT.activation(out=fr[0][:], in_=S[0][0][:], func=AF.Sqrt, bias=self.epsc[:, 0:1]))
                            DVE.wait_ge(s_f4.h, s_f4.base + FQ + 1)
                            self.chain(DVE, DVE.reciprocal(out=fr[0][:], in_=fr[0][:]))
                            s_fin.inc(DVE.scalar_tensor_tensor(out=Onv[:, u, oc], in0=fr[2][:], scalar=gsub[:, 0:1], in1=fr[0][:],
                                                               op0=ALU.mult, op1=ALU.mult))
                        else:
                            for i in range(2):
                                ps = slice(64 * i, 64 * i + 64)
                                self.chain(DVE, DVE.reciprocal(out=fr[i][ps, :], in_=L[i][ps, :]))
                                ins = DVE.tensor_tensor(out=Onv[ps, u, oc], in0=O[i][ps, :], in1=fr[i][ps, :], op=ALU.mult)
                            s_fin.inc(ins)
                        cnt["FQ"] += 1
                    pv_end_unit.append(cnt["I"])
                PE.wait_ge(s_fin.h, s_fin.base + cnt["FQ"])
                wo = wos if grp == "s" else wop
                for st_ in range(QR // P):
                    G = cnt["G"]
                    sl = G % 2
                    if grp == "p":
                        r0 = sq * SPq + q0 + st_ * P
                        if G >= 2:
                            SP.wait_ge(st_o[sl].h, st_o[sl].base + 16 * (G // 2))
                        ld_xr[sl].inc(SP.dma_start(out=xr[sl][:], in_=self.xres[r0:r0 + P, :]), 16)
                    else:
                        r0 = q0 + st_ * P
                        if G >= 2:
                            DVE.wait_ge(st_o[sl].h, st_o[sl].base + 16 * (G // 2))
                    for half in range(2):
                        Yc = cnt["Y"]
                        yb = S[Yc % 2][0]
                        if Yc >= 2:
                            PE.wait_ge(s_ye.h, s_ye.base + Yc - 1)
                        for c in range(nu):
                            ins = PE.matmul(yb[:], Onv[:, c, st_ * P:(st_ + 1) * P], wo[:, c, half * TT:(half + 1) * TT], start=(c == 0), stop=(c == nu - 1))
                        s_ym.inc(ins)
                        DVE.wait_ge(s_ym.h, s_ym.base + Yc + 1)
                        hs_ = slice(half * TT, (half + 1) * TT)
                        if grp == "p":
                            if half == 0:
                                DVE.wait_ge(ld_xr[sl].h, ld_xr[sl].base + ld_xr[sl].v - ld_xr[sl].base)
                            s_ye.inc(DVE.tensor_tensor(out=xr[sl][:, hs_], in0=yb[:], in1=xr[sl][:, hs_], op=ALU.add))
                        else:
                            s_ye.inc(DVE.tensor_copy(out=xr[sl][:, hs_], in_=yb[:]))
                        cnt["Y"] += 1
                    POOL.wait_ge(s_ye.h, s_ye.base + cnt["Y"])
                    if grp == "p":
                        dstap = self.xres[r0:r0 + P, :]
                    else:
                        rk_, wi_ = r0 // self.SC, r0 % self.SC
                        dstap = self.ys_in[wi_ // TT][rk_ * TT + wi_ % TT:rk_ * TT + wi_ % TT + P, :]
                    st_o[sl].inc(POOL.dma_start(out=dstap, in_=xr[sl][:]), 16)
                    cnt["G"] += 1
                PE.wait_ge(s_ye.h, s_ye.base + cnt["Y"])
                if grp == "s" and q0 + QR >= SSq and not self.cfg.get("nors"):
                    for sl in range(2):
                        POOL.wait_ge(st_o[sl].h, st_o[sl].v)
                    for kk in range(self.KP):
                        self.cc.inc(POOL.collective_compute("ReduceScatter", ALU.add, replica_groups=[[0, 1, 2, 3], [4, 5, 6, 7]],
                                                            ins=[self.ys_in[kk][:, :]], outs=[self.ys_out[kk][:, :]]))
            for sl in range(2):
                for e in (SP, POOL, ACT, DVE, PE):
                    e.wait_ge(st_o[sl].h, st_o[sl].v)

    def build(self):
        nc = self.nc
        self.declare()
        self.pools = {}
        self.pool_idx = {}
        self.uid = 0
        self.selfsem = {}
        self.pool_i = 0
        self.st_all = Sem(nc, "st_all")
        self.st_all.base = 0
        self.cc = Sem(nc, "cc")
        self.cc.base = 0
        with ExitStack() as es:
            self.setup_consts(es)
            self.new_phase()
            self.t5_setup()
            if self.depth >= 3:
                self.new_phase()
                self.na_setup()
            src = self.x_in
            mp = self.cfg.get("maxphase", 10 ** 9)
            ph = 0
            for li in range(self.depth):
                kind = "ABC"[li % 3]
                j = li // 3
                steps = [lambda: self.ffn_phase(li * 2, src, "mix", li),
                         lambda: self.qkv_phase(kind, j),
                         lambda: self.attn_phase(kind, j, li),
                         lambda: self.add_phase(),
                         lambda: self.ffn_phase(li * 2 + 1, self.xres, "final" if li == self.depth - 1 else None, li)]
                for fstep in steps:
                    if ph >= mp:
                        break
                    self.new_phase()
                    fstep()
                    ph += 1
                src = self.xres
        return nc

    def new_phase(self):
        self.pool_idx = {}

    def sbt(self, name, shape, dt):
        self.uid += 1
        return self.nc.sbuf_tensor(f"{name}_{self.uid}", shape, dt)

    def pst(self, name, shape, dt):
        self.uid += 1
        return self.nc.psum_tensor(f"{name}_{self.uid}", shape, dt)

    def chain(self, eng, ins):
        key = id(eng)
        if key not in self.selfsem:
            self.selfsem[key] = Sem(self.nc, f"self{len(self.selfsem)}")
        s = self.selfsem[key]
        s.inc(ins)
        eng.wait_ge(s.h, s.v)


def _sem_pooled(self, name, kind=None):
    if kind is None:
        n = name.lower()
        if n in ("cst", "wl", "qwl", "awl") or n.startswith(("stx", "sth", "t5st", "nast", "qstq", "qstv", "asto", "addst")):
            kind = "sw"
        elif n.startswith(("ld", "t5ld", "nald", "qld", "aldx", "addld", "wlh")):
            kind = "hw"
        else:
            kind = "c"
    pool = self.pools.setdefault(kind, [])
    i = self.pool_idx.get(kind, 0)
    if i >= len(pool):
        pool.append(Sem(self.nc, f"s{kind}{len(pool)}"))
    s = pool[i]
    self.pool_idx[kind] = i + 1
    s.base = s.v
    return s


Builder.sem = _sem_pooled


def prep_core_inputs(cfg, inp, consts, c):
    SP, SS, NPC = cfg["SP"], cfg["SS"], cfg["NPC"]
    SC = SS // 4
    rho = c % 4
    f = lambda a: np.ascontiguousarray(np.asarray(a, dtype=np.float32))
    xp = np.asarray(inp["x_prompt"])[NPC * c:NPC * (c + 1)].reshape(NPC * SP, D)
    xs = np.asarray(inp["x_sample"])[c // 4, rho * SC:(rho + 1) * SC]
    m = {}
    m["x_in"] = f(np.concatenate([xp, xs], 0))
    dp = cfg["depth"]
    m["ffn_norm"] = f(np.asarray(inp["ffn_norm"]).reshape(8, D)[:2 * dp])
    m["ffn_w_in"] = f(np.asarray(inp["ffn_w_in"]).reshape(8, D, 2 * DFF)[:2 * dp])
    m["ffn_w_out"] = f(np.asarray(inp["ffn_w_out"]).reshape(8, DFF, D)[:2 * dp])
    m["mix_norm"] = f(np.asarray(inp["mix_norm"])[:dp])
    m["final_norm"] = f(np.asarray(inp["final_norm"]).reshape(1, D))
    aw = np.asarray(inp["a_w_qkv"])
    m["a_w_qkv"] = f(aw)
    sl = slice(256 * rho, 256 * rho + 256)
    m["a_w_qkv_s"] = f(np.concatenate([aw[:, :, 0:1024][:, :, sl], aw[:, :, 1024:2048][:, :, sl], aw[:, :, 2048:3072][:, :, sl]], -1))
    m["a_w_out"] = f(inp["a_w_out"])
    m["a_w_out_s"] = f(np.asarray(inp["a_w_out"])[:, sl, :])
    m["a_lambda"] = f(np.asarray(inp["a_lambda"]).reshape(2, 256))
    m["a_subln"] = f(inp["a_subln"])
    t5 = np.asarray(inp["t5_table"])
    m["t5_all"] = f(np.concatenate([t5, t5[:, 2 * rho:2 * rho + 2]], 1))
    bw = np.asarray(inp["b_w_qkv"])[0]
    q, k, v = bw[:, 0:1024], bw[:, 1024:1280], bw[:, 1280:1536]
    dup = lambda a: np.concatenate([np.concatenate([a[:, 64 * g:64 * g + 64]] * 2, 1) for g in range(4)], 1)
    kd, vd = dup(k), dup(v)
    m["b_w_qkv"] = f(np.concatenate([q, kd, vd], 1))
    m["b_w_qkv_s"] = f(np.concatenate([q[:, sl], kd[:, 128 * rho:128 * rho + 128], vd[:, 128 * rho:128 * rho + 128]], 1))
    qn, kn = np.asarray(inp["b_q_norm"])[0], np.asarray(inp["b_k_norm"])[0]
    m["b_qk_norm"] = f(np.stack([np.concatenate([qn, qn]), np.concatenate([kn, kn])]))
    m["b_w_out"] = f(np.asarray(inp["b_w_out"])[0])
    m["b_w_out_s"] = f(np.asarray(inp["b_w_out"])[0][sl])
    cw = np.asarray(inp["c_w_qkv"])[0]
    m["c_w_qkv"] = f(cw)
    m["c_w_qkv_s"] = f(np.concatenate([cw[:, 0:1024][:, sl], cw[:, 1024:2048][:, sl], cw[:, 2048:3072][:, sl]], 1))
    rpb = np.asarray(inp["c_rpb"])[0]
    rall = np.concatenate([rpb, rpb[4 * rho:4 * rho + 4]], 0)
    rT = np.zeros((32, 20, 16), np.float32)
    rT[0:31, :, 0:15] = np.transpose(rall, (2, 0, 1))
    rT[31, :, :] = NEG
    m["c_rpbT"] = f(rT.reshape(32, 320))
    m["c_w_out"] = f(np.asarray(inp["c_w_out"])[0])
    m["c_w_out_s"] = f(np.asarray(inp["c_w_out"])[0][sl])
    m["k_ident"] = consts["ident"]
    m["k_t5oh"] = consts["t5oh"]
    m["k_blk64"] = consts["blk64"]
    m["k_swap"] = consts["swap"]
    m["k_rope"] = consts["rope"]
    m["k_naoh"] = consts["naoh"]
    return m


def run(cfg, inputs):
    consts = make_consts(cfg)
    b = Builder(cfg)
    nc = b.build()
    in_maps = [prep_core_inputs(cfg, inputs, consts, c) for c in range(8)]
    res = run_bass_kernel_spmd(nc, in_maps, core_ids=list(range(8)))
    SP, SS, NPC = cfg["SP"], cfg["SS"], cfg["NPC"]
    SC = SS // 4
    yp = np.zeros((8 * NPC, SP, D), np.float32)
    ys = np.zeros((2, SS, D), np.float32)
    for c in range(8):
        y = np.asarray(res.results[c]["y_out"])
        yp[NPC * c:NPC * (c + 1)] = y[:NPC * SP].reshape(NPC, SP, D)
        ys[c // 4, (c % 4) * SC:(c % 4 + 1) * SC] = y[NPC * SP:]
    return yp, ys


def kernel(**inputs):
    return run(default_cfg(), inputs)
```

```python
import math
from contextlib import ExitStack

import numpy as np
import jax
import jax.numpy as jnp

import concourse.bass as bass
import concourse.mybir as mybir
from concourse.bass_utils import run_bass_kernel_spmd

F32 = mybir.dt.float32
BF16 = mybir.dt.bfloat16
AF = mybir.ActivationFunctionType
ALU = mybir.AluOpType

P = 128
D = 1024
DFF = 2816
NKC = 8
NJ = 22
TT = 512
EPS = 1e-6
GRID_W = 64
NEG = -30000.0


class Sem:
    def __init__(self, nc, name):
        self.h = nc.alloc_semaphore(name)
        self.v = 0

    def inc(self, ins, n=1):
        ins.then_inc(self.h, n)
        self.v += n
        return self.v


def default_cfg():
    return dict(SP=4096, SS=16384, NPC=2, depth=4)


def t5_bucket_np(rel):
    rel = jnp.asarray(rel, dtype=jnp.int32)
    nb = 16
    max_exact = 8
    ret = jnp.where(rel > 0, nb, 0)
    n = jnp.abs(rel)
    n_f = jnp.maximum(n, 1).astype(jnp.float32)
    large = max_exact + (jnp.log(n_f / max_exact) / math.log(128 / max_exact) * (nb - max_exact)).astype(jnp.int32)
    large = jnp.minimum(large, nb - 1)
    return np.asarray(ret + jnp.where(n < max_exact, n, large))


def make_consts(cfg):
    with jax.default_device(jax.devices("cpu")[0]):
        return _make_consts(cfg)


def _make_consts(cfg):
    SS = cfg["SS"]
    c = {}
    c["ident"] = np.eye(P, dtype=np.float32)
    rel = np.arange(-255, 256)
    bk = t5_bucket_np(rel)
    oh = np.zeros((32, 511 + 256), np.float32)
    oh[bk, np.arange(511)] = 1.0
    bm = int(t5_bucket_np(np.array([-100000]))[0])
    bp = int(t5_bucket_np(np.array([100000]))[0])
    assert int(t5_bucket_np(np.array([-129]))[0]) == bm and int(t5_bucket_np(np.array([129]))[0]) == bp
    oh[bm, 511:511 + 128] = 1.0
    oh[bp, 511 + 128:] = 1.0
    c["t5oh"] = oh
    blk = np.zeros((P, P), np.float32)
    blk[:64, :64] = 1.0 / 64
    blk[64:, 64:] = 1.0 / 64
    c["blk64"] = blk
    sw = np.zeros((P, P), np.float32)
    for p in range(P):
        f32 = p % 32
        partner = p + 16 if f32 < 16 else p - 16
        sw[partner, p] = 1.0
    c["swap"] = sw
    t = jnp.arange(SS, dtype=jnp.int32)
    row, col = t // GRID_W, t % GRID_W
    freqs = 10000.0 ** (-jnp.arange(16, dtype=jnp.float32) / 16)
    ang_r = row.astype(jnp.float32)[:, None] * freqs[None, :]
    ang_c = col.astype(jnp.float32)[:, None] * freqs[None, :]
    cr, sr = np.asarray(jnp.cos(ang_r)), np.asarray(jnp.sin(ang_r))
    cc, sc = np.asarray(jnp.cos(ang_c)), np.asarray(jnp.sin(ang_c))
    rope = np.zeros((2, P, SS), np.float32)
    for p in range(P):
        f = p % 64
        i = f % 16
        sgn = -1.0 if (f % 32) < 16 else 1.0
        if f < 32:
            rope[0, p] = cr[:, i]
            rope[1, p] = sgn * sr[:, i]
        else:
            rope[0, p] = cc[:, i]
            rope[1, p] = sgn * sc[:, i]
    c["rope"] = rope
    colq = np.arange(64)
    cstart = np.clip(colq - 8, 0, 64 - 16)
    naoh = np.zeros((32, 64, P), np.float32)
    for qc in range(64):
        for kc in range(64):
            valid = (kc >= cstart[qc]) and (kc < cstart[qc] + 16)
            if valid:
                ci = int(np.clip(kc - qc + 15, 0, 30))
                naoh[ci, qc, kc] = 1.0
                naoh[ci, qc, kc + 64] = 1.0
            else:
                naoh[31, qc, kc] = 1.0
                naoh[31, qc, kc + 64] = 1.0
    c["naoh"] = naoh
    return c


def na_tile_plan(S):
    R = S // GRID_W
    kr = min(8, R)
    T = S // TT
    tiles = {}
    plan = []
    for t in range(T):
        lst = []
        for j in range(8):
            kc = 4 * t - 2 + j
            if kc < 0 or kc >= S // P:
                continue
            blocks = []
            anyvalid = False
            for a in range(2):
                krow = 2 * kc + a
                for b in range(8):
                    r = 8 * t + b
                    rs = int(np.clip(r - kr // 2, 0, R - kr))
                    if rs <= krow < rs + kr:
                        blocks.append(krow - r + 7)
                        anyvalid = True
                    else:
                        blocks.append(15)
            if not anyvalid:
                continue
            key = tuple(blocks)
            if key not in tiles:
                tiles[key] = len(tiles)
            lst.append((kc, tiles[key]))
        plan.append(lst)
    table = [None] * len(tiles)
    for k, v in tiles.items():
        table[v] = k
    return plan, table


class Builder:
    def __init__(self, cfg):
        self.cfg = cfg
        self.SP, self.SS, self.NPC, self.depth = cfg["SP"], cfg["SS"], cfg["NPC"], cfg["depth"]
        self.SC = self.SS // 4
        self.NPT = self.NPC * self.SP
        self.NTOK = self.NPT + self.SC
        self.nc = bass.Bass("TRN2", target_bir_lowering=False)
        self.nsem = 0
        self.st_all = None
        self.engs = None

    def sem(self, name):
        self.nsem += 1
        return Sem(self.nc, f"{name}_{self.nsem}")

    def din(self, name, shape, dt=F32):
        return self.nc.dram_tensor(name, list(shape), dt, kind="ExternalInput").ap()

    def dscr(self, name, shape, dt):
        return self.nc.dram_tensor(name, list(shape), dt).ap()

    def declare(self):
        nc = self.nc
        dep = self.depth
        self.x_in = self.din("x_in", [self.NTOK, D])
        self.ffn_norm = self.din("ffn_norm", [dep * 2, D])
        self.ffn_w_in = self.din("ffn_w_in", [dep * 2, D, 2 * DFF])
        self.ffn_w_out = self.din("ffn_w_out", [dep * 2, DFF, D])
        self.mix_norm = self.din("mix_norm", [dep, D])
        self.final_norm = self.din("final_norm", [1, D])
        self.a_w_qkv = self.din("a_w_qkv", [2, D, 3072])
        self.a_w_qkv_s = self.din("a_w_qkv_s", [2, D, 768])
        self.a_w_out = self.din("a_w_out", [2, D, D])
        self.a_w_out_s = self.din("a_w_out_s", [2, 256, D])
        self.a_lambda = self.din("a_lambda", [2, 256])
        self.a_subln = self.din("a_subln", [2, 128])
        self.t5_all = self.din("t5_all", [32, 10])
        self.b_w_qkv = self.din("b_w_qkv", [D, 2048])
        self.b_w_qkv_s = self.din("b_w_qkv_s", [D, 512])
        self.b_qk_norm = self.din("b_qk_norm", [2, 128])
        self.b_w_out = self.din("b_w_out", [D, D])
        self.b_w_out_s = self.din("b_w_out_s", [256, D])
        self.c_w_qkv = self.din("c_w_qkv", [D, 3072])
        self.c_w_qkv_s = self.din("c_w_qkv_s", [D, 768])
        self.c_rpbT = self.din("c_rpbT", [32, 20 * 16])
        self.c_w_out = self.din("c_w_out", [D, D])
        self.c_w_out_s = self.din("c_w_out_s", [256, D])
        self.k_ident = self.din("k_ident", [P, P])
        self.k_t5oh = self.din("k_t5oh", [32, 767])
        self.k_blk64 = self.din("k_blk64", [P, P])
        self.k_swap = self.din("k_swap", [P, P])
        self.k_rope = self.din("k_rope", [2, P, self.SS])
        self.k_naoh = self.din("k_naoh", [32, 64, P])
        self.y_out = nc.dram_tensor("y_out", [self.NTOK, D], F32, kind="ExternalOutput").ap()
        self.xres = self.dscr("xres", [self.NTOK, D], F32)
        self.hp = self.dscr("hp", [D, self.NPT], BF16)
        self.KP = self.SC // TT
        self.hs_in = [self.dscr(f"hs_in{k}", [D, TT], BF16) for k in range(self.KP)]
        self.hs_all = [self.dscr(f"hs_all{k}", [4 * D, TT], BF16) for k in range(self.KP)]
        self.qkT_p = self.dscr("qkT_p", [16, P, self.NPT], BF16)
        self.v_p = self.dscr("v_p", [self.NPT, D], BF16)
        self.qkT_s = self.dscr("qkT_s", [4, P, self.SS], BF16)
        self.v_s = self.dscr("v_s", [self.SS, 256], BF16)
        self.ys_in = [self.dscr(f"ys_in{k}", [4 * TT, D], F32) for k in range(self.KP)]
        self.ys_out = [self.dscr(f"ys_out{k}", [TT, D], F32) for k in range(self.KP)]
        self.t5tiles = self.dscr("t5tiles", [3, 10, P, P], F32)
        self.na_plan_p, self.na_table_p = na_tile_plan(self.SP)
        self.na_plan_s, self.na_table_s = na_tile_plan(self.SS)
        self.na_bias_p = self.dscr("na_bias_p", [16, len(self.na_table_p), P, TT], BF16)
        self.na_bias_s = self.dscr("na_bias_s", [4, len(self.na_table_s), P, TT], BF16)

    def phase_begin(self):
        nc = self.nc
        for e in (nc.sync, nc.gpsimd, nc.scalar, nc.vector, nc.tensor):
            e.wait_ge(self.st_all.h, self.st_all.v)
            if self.cc.v:
                e.wait_ge(self.cc.h, self.cc.v)

    def store(self, eng, out, in_):
        return self.st_all.inc(eng.dma_start(out=out, in_=in_), 16)

    def setup_consts(self, es):
        nc = self.nc
        self.ident = es.enter_context(self.sbt("ident", [P, P], BF16))
        self.ones_b = es.enter_context(self.sbt("ones_b", [P, P], BF16))
        self.onesm = es.enter_context(self.sbt("onesm", [P, P], BF16))
        self.epsc = es.enter_context(self.sbt("epsc", [P, 1], F32))
        self.ones_f = es.enter_context(self.sbt("ones_f", [P, P], F32))
        s = self.sem("cst")
        s.inc(nc.gpsimd.dma_start(out=self.ident[:], in_=self.k_ident[:, :]), 16)
        nc.vector.memset(self.ones_b[:], 1.0)
        nc.vector.memset(self.onesm[:], 1.0 / 128)
        nc.vector.memset(self.ones_f[:], 1.0)
        c = self.sem("cstv")
        c.inc(nc.vector.memset(self.epsc[:], EPS))
        for e in (nc.tensor, nc.scalar, nc.vector, nc.gpsimd, nc.sync):
            e.wait_ge(s.h, s.v)
            e.wait_ge(c.h, c.v)

    def t5_setup(self):
        nc = self.nc
        self.phase_begin()
        with ExitStack() as es:
            oh = es.enter_context(self.sbt("t5oh", [32, 767], F32))
            tb = es.enter_context(self.sbt("t5tb", [32, 10], F32))
            stg = [es.enter_context(self.sbt(f"t5stg{i}", [P, 10, 64], F32)) for i in range(2)]
            ps = [es.enter_context(self.pst(f"t5ps{i}", [P, 64, 8], F32)) for i in range(2)]
            ps2 = [es.enter_context(self.pst(f"t5pb{i}", [P, 64, 2], F32)) for i in range(2)]
            ld = self.sem("t5ld")
            ld.inc(nc.sync.dma_start(out=oh[:], in_=self.k_t5oh[:, :]), 16)
            ld.inc(nc.sync.dma_start(out=tb[:], in_=self.t5_all[:, :]), 16)
            nc.tensor.wait_ge(ld.h, ld.v)
            s_mm = self.sem("t5mm")
            s_ev = self.sem("t5ev")
            st = [self.sem("t5st0"), self.sem("t5st1")]
            it = 0
            for d in (-1, 0, 1):
                for qh in range(2):
                    b = it % 2
                    if it >= 2:
                        nc.tensor.wait_ge(s_ev.h, s_ev.base + it - 1)
                    for qi in range(64):
                        q = qh * 64 + qi
                        base = 128 * d + 255 - q
                        nc.tensor.matmul(ps[b][:, qi, :], oh[:, base:base + 128], tb[:, 0:8], start=True, stop=True)
                        mm = nc.tensor.matmul(ps2[b][:, qi, :], oh[:, base:base + 128], tb[:, 8:10], start=True, stop=True)
                    s_mm.inc(mm)
                    nc.vector.wait_ge(s_mm.h, s_mm.base + it + 1)
                    if it >= 2:
                        nc.vector.wait_ge(st[b].h, st[b].base + 16 * (it // 2))
                    nc.vector.tensor_copy(out=stg[b][:, 0:8, :], in_=ps[b][:, :, :].rearrange("p q h -> p h q"))
                    s_ev.inc(nc.vector.tensor_copy(out=stg[b][:, 8:10, :], in_=ps2[b][:, :, :].rearrange("p q h -> p h q")))
                    nc.gpsimd.wait_ge(s_ev.h, s_ev.base + it + 1)
                    ins = nc.gpsimd.dma_start(out=self.t5tiles[d + 1].rearrange("h k q -> k h q")[:, :, qh * 64:(qh + 1) * 64], in_=stg[b][:])
                    st[b].inc(ins, 16)
                    it += 1
            for b in range(2):
                for e in (nc.gpsimd, nc.vector, nc.tensor, nc.sync, nc.scalar):
                    e.wait_ge(st[b].h, st[b].v)

    def na_setup(self):
        nc = self.nc
        self.phase_begin()
        with ExitStack() as es:
            oh = es.enter_context(self.sbt("naoh", [32, 64, P], F32))
            rp = es.enter_context(self.sbt("narp", [32, 320], F32))
            mt = es.enter_context(self.sbt("namt", [P, 20, 16, 64], BF16))
            bt = [es.enter_context(self.sbt(f"nabt{i}", [P, TT], BF16)) for i in range(4)]
            ps = [es.enter_context(self.pst(f"naps{i}", [P, 320], F32)) for i in range(2)]
            ld = self.sem("nald")
            ld.inc(nc.sync.dma_start(out=oh[:], in_=self.k_naoh[:, :, :]), 16)
            ld.inc(nc.sync.dma_start(out=rp[:], in_=self.c_rpbT[:, :]), 16)
            nc.tensor.wait_ge(ld.h, ld.v)
            s_mm = self.sem("namm")
            s_ev = self.sem("naev")
            for qc in range(64):
                b = qc % 2
                if qc >= 2:
                    nc.tensor.wait_ge(s_ev.h, s_ev.base + qc - 1)
                s_mm.inc(nc.tensor.matmul(ps[b][:, :], oh[:, qc, :], rp[:, :], start=True, stop=True))
                nc.vector.wait_ge(s_mm.h, s_mm.base + qc + 1)
                s_ev.inc(nc.vector.tensor_copy(out=mt[:, :, :, qc], in_=ps[b][:, :].rearrange("p (h r) -> p h r", r=16)))
            s_ms = self.sem("nams")
            nc.vector.wait_ge(s_ev.h, s_ev.base + 64)
            s_ms.inc(nc.vector.memset(mt[:, :, 15, :], NEG))
            nc.vector.wait_ge(s_ms.h, s_ms.base + 1)
            s_bt = self.sem("nabt")
            st = [self.sem(f"nast{i}") for i in range(4)]
            it = 0
            for (nh, h0, table, dst) in ((16, 0, self.na_table_p, self.na_bias_p), (4, 16, self.na_table_s, self.na_bias_s)):
                for h in range(nh):
                    for ti, blocks in enumerate(table):
                        b = it % 4
                        if it >= 4:
                            nc.vector.wait_ge(st[b].h, st[b].base + 16 * (it // 4))
                        for a in range(2):
                            for bb in range(8):
                                ri = blocks[a * 8 + bb]
                                ins = nc.vector.tensor_copy(out=bt[b][64 * a:64 * a + 64, bb * 64:(bb + 1) * 64],
                                                            in_=mt[64 * a:64 * a + 64, h0 + h, ri, :])
                        s_bt.inc(ins)
                        nc.gpsimd.wait_ge(s_bt.h, s_bt.base + it + 1)
                        st[b].inc(nc.gpsimd.dma_start(out=dst[h, ti], in_=bt[b][:]), 16)
                        it += 1
            for b in range(4):
                for e in (nc.gpsimd, nc.vector, nc.tensor, nc.sync, nc.scalar):
                    e.wait_ge(st[b].h, st[b].v)

    def tile_order(self, sample_first=True):
        npt = self.NPT // TT
        nst = self.SC // TT
        pt = list(range(npt))
        stl = list(range(npt, npt + nst))
        return (stl + pt) if sample_first else (pt + stl)

    def ffn_phase(self, idx, src, post, li):
        nc = self.nc
        self.phase_begin()
        PE, ACT, DVE, POOL, SP = nc.tensor, nc.scalar, nc.vector, nc.gpsimd, nc.sync
        with ExitStack() as es:
            sb = lambda n, s, d: es.enter_context(self.sbt(n, s, d))
            pb = lambda n, s, d: es.enter_context(self.pst(n, s, d))
            win = sb("win", [P, NKC, 2 * DFF], BF16)
            wout = sb("wout", [P, NJ, D], BF16)
            gb1 = sb("gb1", [P, D], F32)
            gb2 = sb("gb2", [P, D], F32) if post else None
            xin = [sb(f"xin{i}", [P, D], F32) for i in range(2)]
            xsb = [sb(f"xsb{i}", [P, D], BF16) for i in range(4)]
            hT = sb("hT", [P, NKC, TT], BF16)
            actT = sb("actT", [P, NJ, TT], BF16)
            sg = sb("sg", [P, TT], F32)
            xr = [sb(f"xr{i}", [P, D], F32) for i in range(2)]
            ss = sb("ss", [P, 16], F32)
            if post == "mix":
                xsb2 = [sb(f"xsb2{i}", [P, D], BF16) for i in range(2)]
                h2st = [sb(f"h2st{i}", [P, NKC, P], BF16) for i in range(2)]
            tpA = pb("tpA", [P, D], BF16)
            tpB = pb("tpB", [P, D], BF16)
            G = [pb(f"G{i}", [P, TT], F32) for i in range(2)]
            U = [pb(f"U{i}", [P, TT], F32) for i in range(2)]
            Y = [pb(f"Y{i}", [P, TT], F32) for i in range(2)]

            wl = self.sem("wl")
            wsrc = self.ffn_w_in[idx].rearrange("(k p) n -> p k n", p=P)
            for k in range(NKC):
                wl.inc(POOL.dma_start(out=win[:, k, :], in_=wsrc[:, k, :]), 16)
            wsrc2 = self.ffn_w_out[idx].rearrange("(j p) n -> p j n", p=P)
            for j in range(NJ):
                wl.inc(POOL.dma_start(out=wout[:, j, :], in_=wsrc2[:, j, :]), 16)
            wlh = self.sem("wlh")
            wlh.inc(SP.dma_start(out=gb1[:], in_=self.ffn_norm[idx:idx + 1, :].partition_broadcast(P)), 16)
            if post == "mix":
                wlh.inc(SP.dma_start(out=gb2[:], in_=self.mix_norm[li:li + 1, :].partition_broadcast(P)), 16)
            elif post == "final":
                wlh.inc(SP.dma_start(out=gb2[:], in_=self.final_norm[0:1, :].partition_broadcast(P)), 16)
            for e in (PE, DVE):
                e.wait_ge(wl.h, wl.v)
                e.wait_ge(wlh.h, wlh.v)

            ld_x = [self.sem("ldx0"), self.sem("ldx1")]
            ld_xr = [self.sem("ldxr0"), self.sem("ldxr1")]
            st_x = [self.sem("stx0"), self.sem("stx1")]
            st_h2 = [self.sem("sth0"), self.sem("sth1")]
            s_ss, s_xsb, s_tp, s_ht = self.sem("ss"), self.sem("xsb"), self.sem("tp"), self.sem("ht")
            s_gu, s_sg, s_act = self.sem("gu"), self.sem("sg"), self.sem("act")
            s_y, s_res = self.sem("y"), self.sem("res")
            s_ss2, s_xsb2, s_tp2, s_h2 = self.sem("ss2"), self.sem("xsb2"), self.sem("tp2"), self.sem("h2")

            order = self.tile_order(sample_first=True)
            npt = self.NPT // TT
            dst = self.y_out if post == "final" else self.xres

            def post_pe(g):
                PE.wait_ge(s_xsb2.h, s_xsb2.base + g + 1)
                if g >= 1:
                    PE.wait_ge(s_h2.h, s_h2.base + g)
                for c in range(NKC):
                    ins = PE.transpose(out=tpB[:, c * P:(c + 1) * P], in_=xsb2[g % 2][:, c * P:(c + 1) * P], identity=self.ident[:])
                s_tp2.inc(ins)

            for ti, t in enumerate(order):
                t0 = t * TT
                for s in range(4):
                    g = ti * 4 + s
                    r0 = t0 + s * P
                    if g >= 2:
                        SP.wait_ge(s_xsb.h, s_xsb.base + g - 1)
                    ld_x[g % 2].inc(SP.dma_start(out=xin[g % 2][:], in_=src[r0:r0 + P, :]), 16)
                    ACT.wait_ge(ld_x[g % 2].h, ld_x[g % 2].base + 16 * (g // 2 + 1))
                    if g >= 4:
                        ACT.wait_ge(s_tp.h, s_tp.base + g - 3)
                    self.chain(ACT, ACT.activation(out=xsb[s][:], in_=xin[g % 2][:], func=AF.Square, accum_out=ss[:, s:s + 1]))
                    s_ss.inc(ACT.activation(out=ss[:, 4 + s:5 + s], in_=ss[:, s:s + 1], func=AF.Sqrt, scale=1.0 / D, bias=self.epsc[:, 0:1]))
                    DVE.wait_ge(s_ss.h, s_ss.base + g + 1)
                    self.chain(DVE, DVE.reciprocal(out=ss[:, 4 + s:5 + s], in_=ss[:, 4 + s:5 + s]))
                    s_xsb.inc(DVE.scalar_tensor_tensor(out=xsb[s][:], in0=xin[g % 2][:], scalar=ss[:, 4 + s:5 + s], in1=gb1[:],
                                                       op0=ALU.mult, op1=ALU.mult))
                    PE.wait_ge(s_xsb.h, s_xsb.base + g + 1)
                    if g >= 1:
                        PE.wait_ge(s_ht.h, s_ht.base + g)
                    for c in range(NKC):
                        ins = PE.transpose(out=tpA[:, c * P:(c + 1) * P], in_=xsb[s][:, c * P:(c + 1) * P], identity=self.ident[:])
                    s_tp.inc(ins)
                    ACT.wait_ge(s_tp.h, s_tp.base + g + 1)
                    s_ht.inc(ACT.activation(out=hT[:, :, s * P:(s + 1) * P], in_=tpA[:, :].rearrange("p (c t) -> p c t", c=NKC), func=AF.Copy))
                PE.wait_ge(s_ht.h, s_ht.base + 4 * (ti + 1))
                for j in range(NJ):
                    J = ti * NJ + j
                    if J >= 2:
                        PE.wait_ge(s_act.h, s_act.base + J - 1)
                    for k in range(NKC):
                        PE.matmul(G[J % 2][:], win[:, k, j * P:(j + 1) * P], hT[:, k, :], start=(k == 0), stop=(k == NKC - 1))
                    for k in range(NKC):
                        ins = PE.matmul(U[J % 2][:], win[:, k, DFF + j * P:DFF + (j + 1) * P], hT[:, k, :], start=(k == 0), stop=(k == NKC - 1))
                    s_gu.inc(ins)
                    ACT.wait_ge(s_gu.h, s_gu.base + J + 1)
                    if J >= 1:
                        ACT.wait_ge(s_act.h, s_act.base + J)
                    s_sg.inc(ACT.activation(out=sg[:], in_=G[J % 2][:], func=AF.Silu))
                    DVE.wait_ge(s_sg.h, s_sg.base + J + 1)
                    s_act.inc(DVE.tensor_tensor(out=actT[:, j, :], in0=sg[:], in1=U[J % 2][:], op=ALU.mult))
                PE.wait_ge(s_act.h, s_act.base + NJ * (ti + 1))
                for s in range(4):
                    g = ti * 4 + s
                    r0 = t0 + s * P
                    slot = g % 2
                    if g >= 2:
                        SP.wait_ge(st_x[slot].h, st_x[slot].base + 16 * (g // 2))
                        if post == "mix":
                            SP.wait_ge(s_xsb2.h, s_xsb2.base + g - 1)
                    ld_xr[slot].inc(SP.dma_start(out=xr[slot][:], in_=src[r0:r0 + P, :]), 16)
                    for half in range(2):
                        g2 = g * 2 + half
                        if g2 >= 2:
                            PE.wait_ge(s_res.h, s_res.base + g2 - 1)
                        for j in range(NJ):
                            ins = PE.matmul(Y[g2 % 2][:], actT[:, j, s * P:(s + 1) * P], wout[:, j, half * TT:(half + 1) * TT],
                                            start=(j == 0), stop=(j == NJ - 1))
                        s_y.inc(ins)
                        DVE.wait_ge(s_y.h, s_y.base + g2 + 1)
                        if half == 0:
                            DVE.wait_ge(ld_xr[slot].h, ld_xr[slot].base + 16 * (g // 2 + 1))
                        s_res.inc(DVE.scalar_tensor_tensor(out=xr[slot][:, half * TT:(half + 1) * TT], in0=Y[g2 % 2][:], scalar=0.5,
                                                           in1=xr[slot][:, half * TT:(half + 1) * TT], op0=ALU.mult, op1=ALU.add))
                    if post == "mix" and s >= 1:
                        post_pe(g - 1)
                    if post != "final":
                        POOL.wait_ge(s_res.h, s_res.base + 2 * g + 2)
                        ins = POOL.dma_start(out=dst[r0:r0 + P, :], in_=xr[slot][:])
                        st_x[slot].inc(ins, 16)
                    if post == "mix":
                        ACT.wait_ge(s_res.h, s_res.base + 2 * g + 2)
                        if g >= 2:
                            ACT.wait_ge(s_tp2.h, s_tp2.base + g - 1)
                        self.chain(ACT, ACT.activation(out=xsb2[slot][:], in_=xr[slot][:], func=AF.Square, accum_out=ss[:, 8 + slot:9 + slot]))
                        s_ss2.inc(ACT.activation(out=ss[:, 12 + slot:13 + slot], in_=ss[:, 8 + slot:9 + slot], func=AF.Sqrt, scale=1.0 / D, bias=self.epsc[:, 0:1]))
                        DVE.wait_ge(s_ss2.h, s_ss2.base + g + 1)
                        self.chain(DVE, DVE.reciprocal(out=ss[:, 12 + slot:13 + slot], in_=ss[:, 12 + slot:13 + slot]))
                        s_xsb2.inc(DVE.scalar_tensor_tensor(out=xsb2[slot][:], in0=xr[slot][:], scalar=ss[:, 12 + slot:13 + slot], in1=gb2[:],
                                                            op0=ALU.mult, op1=ALU.mult))
                        if s == 3:
                            post_pe(g)
                        for gg in ([g - 1] if 1 <= s < 3 else ([g - 1, g] if s == 3 else [])):
                            sl = gg % 2
                            ACT.wait_ge(s_tp2.h, s_tp2.base + gg + 1)
                            if gg >= 2:
                                ACT.wait_ge(st_h2[sl].h, st_h2[sl].base + 16 * (gg // 2))
                            s_h2.inc(ACT.activation(out=h2st[sl][:], in_=tpB[:, :].rearrange("p (c t) -> p c t", c=NKC), func=AF.Copy))
                            POOL.wait_ge(s_h2.h, s_h2.base + gg + 1)
                            tg = order[gg // 4]
                            rr = tg * TT + (gg % 4) * P
                            if tg < npt:
                                hd = self.hp.rearrange("(c p) t -> p c t", p=P)[:, :, rr:rr + P]
                            else:
                                rr -= self.NPT
                                hd = self.hs_in[rr // TT].rearrange("(c p) t -> p c t", p=P)[:, :, rr % TT:rr % TT + P]
                            ins = POOL.dma_start(out=hd, in_=h2st[sl][:])
                            st_h2[sl].inc(ins, 16)
                    elif post == "final":
                        ACT.wait_ge(s_res.h, s_res.base + 2 * g + 2)
                        self.chain(ACT, ACT.activation(out=xsb[s][:], in_=xr[slot][:], func=AF.Square, accum_out=ss[:, 8 + slot:9 + slot]))
                        s_ss2.inc(ACT.activation(out=ss[:, 12 + slot:13 + slot], in_=ss[:, 8 + slot:9 + slot], func=AF.Sqrt, scale=1.0 / D, bias=self.epsc[:, 0:1]))
                        DVE.wait_ge(s_ss2.h, s_ss2.base + g + 1)
                        self.chain(DVE, DVE.reciprocal(out=ss[:, 12 + slot:13 + slot], in_=ss[:, 12 + slot:13 + slot]))
                        s_xsb2.inc(DVE.scalar_tensor_tensor(out=xr[slot][:], in0=xr[slot][:], scalar=ss[:, 12 + slot:13 + slot], in1=gb2[:],
                                                            op0=ALU.mult, op1=ALU.mult))
                        POOL.wait_ge(s_xsb2.h, s_xsb2.base + g + 1)
                        st_x[slot].inc(POOL.dma_start(out=dst[r0:r0 + P, :], in_=xr[slot][:]), 16)
                if post == "mix" and ti < (self.SC // TT):
                    for sl in range(2):
                        POOL.wait_ge(st_h2[sl].h, st_h2[sl].v)
                    kk = t - npt
                    self.cc.inc(POOL.collective_compute("AllGather", ALU.bypass, replica_groups=[[0, 1, 2, 3], [4, 5, 6, 7]],
                                                        ins=[self.hs_in[kk][:, :]], outs=[self.hs_all[kk][:, :]]))
            for sl in range(2):
                for e in (SP, POOL, ACT, DVE, PE):
                    e.wait_ge(st_x[sl].h, st_x[sl].v)
                    if post == "mix":
                        e.wait_ge(st_h2[sl].h, st_h2[sl].v)

    def add_phase(self):
        nc = self.nc
        self.phase_begin()
        SP, DVE, POOL = nc.sync, nc.vector, nc.gpsimd
        with ExitStack() as es:
            a = [es.enter_context(self.sbt(f"adda{i}", [P, D], F32)) for i in range(2)]
            b = [es.enter_context(self.sbt(f"addb{i}", [P, D], F32)) for i in range(2)]
            ld = [self.sem("addld0"), self.sem("addld1")]
            st = [self.sem("addst0"), self.sem("addst1")]
            s_add = self.sem("add")
            n = self.SC // P
            for g in range(n):
                sl = g % 2
                if g >= 2:
                    SP.wait_ge(st[sl].h, st[sl].base + 16 * (g // 2))
                ld[sl].inc(SP.dma_start(out=a[sl][:], in_=self.xres[self.NPT + g * P:self.NPT + (g + 1) * P, :]), 16)
                ld[sl].inc(SP.dma_start(out=b[sl][:], in_=self.ys_out[(g * P) // TT][(g * P) % TT:(g * P) % TT + P, :]), 16)
                DVE.wait_ge(ld[sl].h, ld[sl].base + 32 * (g // 2 + 1))
                s_add.inc(DVE.tensor_tensor(out=a[sl][:], in0=a[sl][:], in1=b[sl][:], op=ALU.add))
                POOL.wait_ge(s_add.h, s_add.base + g + 1)
                st[sl].inc(POOL.dma_start(out=self.xres[self.NPT + g * P:self.NPT + (g + 1) * P, :], in_=a[sl][:]), 16)
            for sl in range(2):
                for e in (SP, POOL, DVE, nc.scalar, nc.tensor):
                    e.wait_ge(st[sl].h, st[sl].v)

    def qkv_phase(self, kind, j):
        nc = self.nc
        self.phase_begin()
        PE, ACT, DVE, POOL, SP = nc.tensor, nc.scalar, nc.vector, nc.gpsimd, nc.sync
        if kind == "A":
            wp, ws = self.a_w_qkv[j], self.a_w_qkv_s[j]
            ncp, ncs = 3072, 768
            qk_p = [(h * P, 0.125) for h in range(8)] + [(1024 + h * P, 1.0) for h in range(8)]
            qk_s = [(h * P, 0.125) for h in range(2)] + [(256 + h * P, 1.0) for h in range(2)]
            v_p, v_s = (2048, 1024), (512, 256)
        elif kind == "B":
            wp, ws = self.b_w_qkv, self.b_w_qkv_s
            ncp, ncs = 2048, 512
            qk_p = [(c * P, "q") for c in range(8)] + [(1024 + g * P, "k") for g in range(4)]
            qk_s = [(c * P, "q") for c in range(2)] + [(256, "k")]
            v_p, v_s = (1536, 512), (384, 128)
        else:
            wp, ws = self.c_w_qkv, self.c_w_qkv_s
            ncp, ncs = 3072, 768
            qk_p = [(c * P, 0.125) for c in range(8)] + [(1024 + c * P, 1.0) for c in range(8)]
            qk_s = [(c * P, 0.125) for c in range(2)] + [(256 + c * P, 1.0) for c in range(2)]
            v_p, v_s = (2048, 1024), (512, 256)
        with ExitStack() as es:
            sb = lambda n, s, d: es.enter_context(self.sbt(n, s, d))
            pb = lambda n, s, d: es.enter_context(self.pst(n, s, d))
            wpt = sb("wqp", [P, NKC, ncp], BF16)
            wst = sb("wqs", [P, NKC, ncs], BF16)
            hT = [sb(f"qhT{i}", [P, NKC, TT], BF16) for i in range(2)]
            qkst = [sb(f"qkst{i}", [P, 16, TT], BF16) for i in range(2)]
            vst = [sb(f"vst{i}", [P, 4, 1024], BF16) for i in range(2)]
            acc = [pb(f"qacc{i}", [P, TT], F32) for i in range(2)]
            vacc = [pb(f"vacc{i}", [P, TT], F32) for i in range(2)]
            if kind == "B":
                gn = sb("bgn", [P, 2], F32)
                blk = sb("bblk", [P, P], BF16)
                swp = sb("bswp", [P, P], BF16)
                rope = [sb(f"rope{i}", [P, 2, TT], F32) for i in range(2)]
                sqb = sb("bsq", [P, TT], BF16)
                rr = sb("brr", [P, TT], F32)
                qnb = sb("bqn", [P, TT], BF16)
                t1 = sb("bt1", [P, TT], F32)
                t2 = sb("bt2", [P, TT], F32)
                msq = pb("bmsq", [P, TT], F32)
                rot = pb("brot", [P, TT], F32)
            wl = self.sem("qwl")
            for k in range(NKC):
                wl.inc(POOL.dma_start(out=wpt[:, k, :], in_=wp.rearrange("(k p) n -> p k n", p=P)[:, k, :]), 16)
            wl.inc(POOL.dma_start(out=wst[:], in_=ws.rearrange("(k p) n -> p k n", p=P)), 16)
            if kind == "B":
                wl.inc(POOL.dma_start(out=blk[:], in_=self.k_blk64[:, :]), 16)
                wl.inc(POOL.dma_start(out=swp[:], in_=self.k_swap[:, :]), 16)
                wlh = self.sem("wlh")
                wlh.inc(SP.dma_start(out=gn[:, 0:1], in_=self.b_qk_norm[0:1, :].rearrange("a p -> p a")), 16)
                wlh.inc(SP.dma_start(out=gn[:, 1:2], in_=self.b_qk_norm[1:2, :].rearrange("a p -> p a")), 16)
                DVE.wait_ge(wlh.h, wlh.v)
            for e in (PE, DVE):
                e.wait_ge(wl.h, wl.v)
            ld_h = [self.sem("qldh0"), self.sem("qldh1")]
            ld_r = [self.sem("qldr0"), self.sem("qldr1")]
            st_q = [self.sem("qstq0"), self.sem("qstq1")]
            st_v = [self.sem("qstv0"), self.sem("qstv1")]
            s_mm, s_ev, s_vmm, s_vev, s_done = self.sem("qmm"), self.sem("qev"), self.sem("qvmm"), self.sem("qvev"), self.sem("qdone")
            s_b = [self.sem(f"qb{i}") for i in range(6)]
            npt = self.NPT // TT
            nst = self.SS // TT
            nsc = self.SC // TT
            tiles = [("s", i) for i in range(nst)] + [("p", i) for i in range(npt)]
            vend = []
            mend = []
            stq_after, stv_after = [], []
            M = 0
            V = 0
            for ti, (grp, i) in enumerate(tiles):
                sl = ti % 2
                if grp == "p":
                    hsrc = self.hp.rearrange("(c p) t -> p c t", p=P)[:, :, i * TT:(i + 1) * TT]
                    w, qk, (vc0, vw) = wpt, qk_p, v_p
                    qdst = self.qkT_p[0:len(qk_p)].rearrange("m p t -> p m t")[:, :, i * TT:(i + 1) * TT]
                    vdst = self.v_p[i * TT:(i + 1) * TT, 0:vw].rearrange("(s p) e -> p s e", p=P)
                    pos0 = (i * TT) % self.SP
                else:
                    r = i // nsc
                    hsrc = self.hs_all[i % nsc].rearrange("(r c p) t -> p r c t", p=P, c=NKC)[:, r, :, :]
                    w, qk, (vc0, vw) = wst, qk_s, v_s
                    qdst = self.qkT_s[0:len(qk_s)].rearrange("m p t -> p m t")[:, :, i * TT:(i + 1) * TT]
                    vdst = self.v_s[i * TT:(i + 1) * TT, 0:vw].rearrange("(s p) e -> p s e", p=P)
                    pos0 = i * TT
                if ti >= 2:
                    SP.wait_ge(s_vmm.h, s_vmm.base + vend[ti - 2])
                ld_h[sl].inc(SP.dma_start(out=hT[sl][:], in_=hsrc), 16)
                if kind == "B":
                    if ti >= 2:
                        SP.wait_ge(s_ev.h, s_ev.base + mend[ti - 2])
                    ld_r[sl].inc(SP.dma_start(out=rope[sl][:], in_=self.k_rope.rearrange("a p t -> p a t")[:, :, pos0:pos0 + TT]), 16)
                    DVE.wait_ge(ld_r[sl].h, ld_r[sl].base + 16 * (ti // 2 + 1))
                PE.wait_ge(ld_h[sl].h, ld_h[sl].base + 16 * (ti // 2 + 1))
                if ti >= 2:
                    for e in (ACT, DVE):
                        e.wait_ge(st_q[sl].h, stq_after[ti - 2])
                        e.wait_ge(st_v[sl].h, stv_after[ti - 2])
                for m, (c0, mode) in enumerate(qk):
                    a = M % 2
                    if M >= 2:
                        PE.wait_ge(s_ev.h, s_ev.base + M - 1)
                    for k in range(NKC):
                        ins = PE.matmul(acc[a][:], w[:, k, c0:c0 + P], hT[sl][:, k, :], start=(k == 0), stop=(k == NKC - 1))
                    s_mm.inc(ins)
                    if kind != "B":
                        ACT.wait_ge(s_mm.h, s_mm.base + M + 1)
                        s_ev.inc(ACT.activation(out=qkst[sl][:, m, :], in_=acc[a][:], func=AF.Copy, scale=float(mode)))
                    else:
                        gi = 0 if mode == "q" else 1
                        ACT.wait_ge(s_mm.h, s_mm.base + M + 1)
                        if M >= 1:
                            ACT.wait_ge(s_b[1].h, s_b[1].base + M)
                        s_b[0].inc(ACT.activation(out=sqb[:], in_=acc[a][:], func=AF.Square))
                        PE.wait_ge(s_b[0].h, s_b[0].base + M + 1)
                        if M >= 1:
                            PE.wait_ge(s_b[2].h, s_b[2].base + M)
                        s_b[1].inc(PE.matmul(msq[:], blk[:], sqb[:], start=True, stop=True))
                        ACT.wait_ge(s_b[1].h, s_b[1].base + M + 1)
                        if M >= 1:
                            ACT.wait_ge(s_b[2].h, s_b[2].base + M)
                        s_b[3].inc(ACT.activation(out=rr[:], in_=msq[:], func=AF.Sqrt, bias=self.epsc[:, 0:1]))
                        DVE.wait_ge(s_b[3].h, s_b[3].base + M + 1)
                        self.chain(DVE, DVE.reciprocal(out=rr[:], in_=rr[:]))
                        if M >= 1:
                            DVE.wait_ge(s_b[4].h, s_b[4].base + M)
                        s_b[2].inc(DVE.scalar_tensor_tensor(out=qnb[:], in0=acc[a][:], scalar=gn[:, gi:gi + 1], in1=rr[:],
                                                            op0=ALU.mult, op1=ALU.mult))
                        PE.wait_ge(s_b[2].h, s_b[2].base + M + 1)
                        if M >= 1:
                            PE.wait_ge(s_ev.h, s_ev.base + M)
                        s_b[4].inc(PE.matmul(rot[:], swp[:], qnb[:], start=True, stop=True))
                        DVE.wait_ge(s_b[2].h, s_b[2].base + M + 1)
                        DVE.tensor_tensor(out=t1[:], in0=qnb[:], in1=rope[sl][:, 0, :], op=ALU.mult)
                        DVE.wait_ge(s_b[4].h, s_b[4].base + M + 1)
                        self.chain(DVE, DVE.tensor_tensor(out=t2[:], in0=rot[:], in1=rope[sl][:, 1, :], op=ALU.mult))
                        s_ev.inc(DVE.tensor_tensor(out=qkst[sl][:, m, :], in0=t1[:], in1=t2[:], op=ALU.add))
                    M += 1
                mend.append(M)
                POOL.wait_ge(s_ev.h, s_ev.base + M)
                for m0 in range(0, len(qk), 8):
                    m1 = min(len(qk), m0 + 8)
                    st_q[sl].inc(POOL.dma_start(out=qdst[:, m0:m1, :], in_=qkst[sl][:, m0:m1, :]), 16)
                stq_after.append(st_q[sl].v)
                nh = (vw + TT - 1) // TT
                for s in range(4):
                    for hh in range(nh):
                        a = V % 2
                        wv = min(TT, vw - hh * TT)
                        if V >= 2:
                            PE.wait_ge(s_vev.h, s_vev.base + V - 1)
                        for k in range(NKC):
                            ins = PE.matmul(vacc[a][:, 0:wv], hT[sl][:, k, s * P:(s + 1) * P], w[:, k, vc0 + hh * TT:vc0 + hh * TT + wv],
                                            start=(k == 0), stop=(k == NKC - 1))
                        s_vmm.inc(ins)
                        ACT.wait_ge(s_vmm.h, s_vmm.base + V + 1)
                        s_vev.inc(ACT.activation(out=vst[sl][:, s, hh * TT:hh * TT + wv], in_=vacc[a][:, 0:wv], func=AF.Copy))
                        V += 1
                POOL.wait_ge(s_vev.h, s_vev.base + V)
                st_v[sl].inc(POOL.dma_start(out=vdst, in_=vst[sl][:, :, 0:vw]), 16)
                stv_after.append(st_v[sl].v)
                vend.append(V)
            for sl in range(2):
                for e in (SP, POOL, ACT, DVE, PE):
                    e.wait_ge(st_q[sl].h, st_q[sl].v)
                    e.wait_ge(st_v[sl].h, st_v[sl].v)

    def attn_phase(self, kind, j, li):
        nc = self.nc
        self.phase_begin()
        PE, ACT, DVE, POOL, SP = nc.tensor, nc.scalar, nc.vector, nc.gpsimd, nc.sync
        SPq, SSq = self.SP, self.SS
        QRP, QRS = min(2048, SPq), min(8192, SSq)
        lam_init = 0.8 - 0.6 * math.exp(-0.3 * li)
        if kind == "A":
            wo_p, wo_s = self.a_w_out[j], self.a_w_out_s[j]
        elif kind == "B":
            wo_p, wo_s = self.b_w_out, self.b_w_out_s
        else:
            wo_p, wo_s = self.c_w_out, self.c_w_out_s
        with ExitStack() as es:
            sb = lambda n, s, d: es.enter_context(self.sbt(n, s, d))
            pb = lambda n, s, d: es.enter_context(self.pst(n, s, d))
            On = sb("On", [P, max(8 * QRP, 2 * QRS)], BF16)
            kt = sb("kt", [P, max(SPq, SSq)], BF16)
            vt = sb("vt", [P, max(SPq, SSq) // P, P], BF16)
            qtl = [sb(f"qtl{i}", [P, TT], BF16) for i in range(2)]
            pt = [[sb(f"pt{a}{i}", [P, TT], BF16) for i in range(2)] for a in range(2)]
            tmp = [sb(f"tmp{i}", [P, TT], F32) for i in range(2)]
            fr = [sb(f"fr{i}", [P, TT], F32) for i in range(3)]
            sqf = sb("sqf", [P, TT], BF16)
            Lacc = [sb(f"Lacc{i}", [P, TT], F32) for i in range(2)]
            wop = sb("wop", [P, 8, D], BF16)
            wos = sb("wos", [P, 2, D], BF16)
            xr = [sb(f"axr{i}", [P, D], F32) for i in range(2)]
            S = [[pb(f"S{a}{i}", [P, TT], F32) for i in range(2)] for a in range(2)]
            O = [pb(f"O{i}", [P, TT], F32) for i in range(2)]
            L = [pb(f"L{i}", [P, TT], F32) for i in range(2)]
            wl = self.sem("awl")
            wl.inc(POOL.dma_start(out=wop[:], in_=wo_p.rearrange("(c p) n -> p c n", p=P)), 16)
            wl.inc(POOL.dma_start(out=wos[:], in_=wo_s.rearrange("(c p) n -> p c n", p=P)), 16)
            if kind == "A":
                t5t = sb("t5t", [P, 3, P], F32)
                cb = sb("cb", [P, 2, 10], F32)
                lam = sb("lam", [P, 256], F32)
                lsc = sb("lsc", [P, 8], F32)
                gsub = sb("gsub", [P, 1], F32)
                oh = sb("aoh", [32, 256], F32)
                tb = sb("atb", [32, 10], F32)
                wlh = self.sem("wlh")
                wlh.inc(SP.dma_start(out=oh[:], in_=self.k_t5oh[:, 511:767]), 16)
                wlh.inc(SP.dma_start(out=tb[:], in_=self.t5_all[:, :]), 16)
                wlh.inc(SP.dma_start(out=lam[:], in_=self.a_lambda[j:j + 1, :].partition_broadcast(P)), 16)
                wlh.inc(SP.dma_start(out=gsub[:], in_=self.a_subln[j:j + 1, :].rearrange("o p -> p o")), 16)
                for e in (PE, DVE, ACT):
                    e.wait_ge(wlh.h, wlh.v)
            if kind == "C":
                ntp, nts = len(self.na_table_p), len(self.na_table_s)
                nab = sb("nab", [P, 2, max(ntp, nts), TT], BF16)
            for e in (PE, DVE, ACT):
                e.wait_ge(wl.h, wl.v)
            s_pre = self.sem("apre")
            if kind == "A":
                for sg_ in range(2):
                    PE.matmul(S[0][sg_][:, 0:10], oh[:, sg_ * P:(sg_ + 1) * P], tb[:, :], start=True, stop=True)
                s_pre.inc(PE.matmul(S[1][0][:, 0:10], oh[:, 0:P], tb[:, :], start=True, stop=True))
                DVE.wait_ge(s_pre.h, s_pre.base + 1)
                DVE.tensor_copy(out=cb[:, 0, :], in_=S[0][0][:, 0:10])
                DVE.tensor_copy(out=cb[:, 1, :], in_=S[0][1][:, 0:10])
                DVE.tensor_tensor(out=lam[:, 0:64], in0=lam[:, 0:64], in1=lam[:, 64:128], op=ALU.mult)
                self.chain(DVE, DVE.tensor_tensor(out=lam[:, 128:192], in0=lam[:, 128:192], in1=lam[:, 192:256], op=ALU.mult))
                DVE.reduce_sum(out=lsc[:, 0:1], in_=lam[:, 0:64], axis=mybir.AxisListType.X)
                s_pre.inc(DVE.reduce_sum(out=lsc[:, 1:2], in_=lam[:, 128:192], axis=mybir.AxisListType.X))
                ACT.wait_ge(s_pre.h, s_pre.base + 2)
                s_pre.inc(ACT.activation(out=lsc[:, 2:4], in_=lsc[:, 0:2], func=AF.Exp))
                DVE.wait_ge(s_pre.h, s_pre.base + 3)
                self.chain(DVE, DVE.tensor_tensor(out=lsc[:, 4:5], in0=lsc[:, 3:4], in1=lsc[:, 2:3], op=ALU.subtract))
                DVE.tensor_scalar(out=lsc[:, 5:6], in0=lsc[:, 4:5], scalar1=-lam_init, scalar2=None, op0=ALU.add)
                s_pre.inc(DVE.tensor_scalar(out=gsub[:], in0=gsub[:], scalar1=1.0 - lam_init, scalar2=None, op0=ALU.mult))
                for e in (PE, ACT, DVE):
                    e.wait_ge(s_pre.h, s_pre.base + 4)
                neg_lam = lsc[:, 5:6]

            ld_kv, ld_b = self.sem("ldkv"), self.sem("ldb")
            ld_q = [self.sem("ldq0"), self.sem("ldq1")]
            ld_xr = [self.sem("aldx0"), self.sem("aldx1")]
            st_o = [self.sem("asto0"), self.sem("asto1")]
            s_qk, s_tmp, s_p, s_pv = self.sem("qk"), self.sem("tmp"), self.sem("p"), self.sem("pv")
            s_f1, s_f2, s_f3, s_f4, s_fin = self.sem("f1"), self.sem("f2"), self.sem("f3"), self.sem("f4"), self.sem("fin")
            s_ym, s_ye = self.sem("ym"), self.sem("ye")
            s_l = [self.sem("lacc0"), self.sem("lacc1")]
            s_lm = self.sem("lmm")
            cnt = dict(I=0, QT=0, U=0, NT=0, FQ=0, Y=0, G=0)
            qk_end = []
            pv_end_unit = [0]

            jobs = []
            for q0 in range(0, SSq, QRS):
                jobs.append(("s", 0, q0, QRS))
            for sq in range(self.NPC):
                for q0 in range(0, SPq, QRP):
                    jobs.append(("p", sq, q0, QRP))

            for (grp, sq, q0, QR) in jobs:
                Sk = SSq if grp == "s" else SPq
                nu = 2 if grp == "s" else 8
                tok0 = 0 if grp == "s" else sq * SPq
                qkT = self.qkT_s if grp == "s" else self.qkT_p
                vsrc = self.v_s if grp == "s" else self.v_p
                nkc = Sk // P
                Onv = On[:, 0:nu * QR].rearrange("p (u t) -> p u t", u=nu)
                for u in range(nu):
                    if kind == "B":
                        kidx = (2 if grp == "s" else 8) + u // 2
                        vcol = (u // 2) * P if grp == "p" else 0
                    else:
                        kidx = (2 if grp == "s" else 8) + u
                        vcol = u * P
                    Uc = cnt["U"]
                    SP.wait_ge(s_pv.h, s_pv.base + pv_end_unit[-1])
                    SP.wait_ge(s_fin.h, s_fin.base + cnt["FQ"])
                    ld_kv.inc(SP.dma_start(out=kt[:, 0:Sk], in_=qkT[kidx, :, tok0:tok0 + Sk]), 16)
                    vview = vsrc[tok0:tok0 + Sk, vcol:vcol + P].rearrange("(c p) e -> p c e", p=P)
                    for c0_ in range(0, nkc, 16):
                        c1_ = min(nkc, c0_ + 16)
                        ld_kv.inc(SP.dma_start(out=vt[:, c0_:c1_, :], in_=vview[:, c0_:c1_, :]), 16)
                    PE.wait_ge(ld_kv.h, ld_kv.v)
                    if kind == "A":
                        hcol = (8 + u) if grp == "s" else u
                        ld_b.inc(SP.dma_start(out=t5t[:], in_=self.t5tiles[:, hcol].rearrange("d k q -> k d q")), 16)
                        DVE.wait_ge(ld_b.h, ld_b.v)
                    if kind == "C":
                        nbs = self.na_bias_s if grp == "s" else self.na_bias_p
                        ntl = nts if grp == "s" else ntp
                        for hh_ in range(2):
                            for n0_ in range(0, ntl, 4):
                                n1_ = min(ntl, n0_ + 4)
                                ld_b.inc(SP.dma_start(out=nab[:, hh_, n0_:n1_, :], in_=nbs[2 * u + hh_, n0_:n1_].rearrange("n k q -> k n q")), 16)
                        DVE.wait_ge(ld_b.h, ld_b.v)
                    cnt["U"] += 1
                    for qtl_i in range(QR // TT):
                        qpos = q0 + qtl_i * TT
                        tq = qpos // TT
                        QT = cnt["QT"]
                        qs = QT % 2
                        if QT >= 2:
                            SP.wait_ge(s_qk.h, s_qk.base + qk_end[QT - 2])
                        ld_q[qs].inc(SP.dma_start(out=qtl[qs][:], in_=qkT[u, :, tok0 + qpos:tok0 + qpos + TT]), 16)
                        PE.wait_ge(ld_q[qs].h, ld_q[qs].base + 16 * (QT // 2 + 1))
                        if kind == "C":
                            plan = (self.na_plan_s if grp == "s" else self.na_plan_p)[tq]
                            chunks = [(kc, "tile", tid) for (kc, tid) in plan]
                        elif kind == "B":
                            chunks = [(kc, "plain", None) for kc in range(nkc)]
                        else:
                            chunks = []
                            for kc in range(nkc):
                                ds_ = [kc - (4 * tq + b) for b in range(4)]
                                if all(d <= -2 for d in ds_):
                                    chunks.append((kc, "far", 0))
                                elif all(d >= 2 for d in ds_):
                                    chunks.append((kc, "far", 1))
                                else:
                                    chunks.append((kc, "near", ds_))
                        PE.wait_ge(s_fin.h, s_fin.base + cnt["FQ"])
                        n = len(chunks)

                        def emit_qk(ci):
                            I = cnt["I"] + ci
                            kc = chunks[ci][0]
                            a = I % 2
                            if I >= 2:
                                PE.wait_ge(s_p.h, s_p.base + I - 1)
                            PE.matmul(S[a][0][:], kt[0:64, kc * P:(kc + 1) * P], qtl[qs][0:64, :], start=True, stop=True)
                            s_qk.inc(PE.matmul(S[a][1][:], kt[64:128, kc * P:(kc + 1) * P], qtl[qs][64:128, :], start=True, stop=True))

                        emit_qk(0)
                        for ci in range(n):
                            I = cnt["I"] + ci
                            a = I % 2
                            kc, mode, info = chunks[ci]
                            if ci + 1 < n:
                                emit_qk(ci + 1)
                            if I >= 2:
                                ACT.wait_ge(s_pv.h, s_pv.base + I - 1)
                                ACT.wait_ge(s_l[0].h, s_l[0].base + I - 1)
                            if mode in ("plain", "far"):
                                ACT.wait_ge(s_qk.h, s_qk.base + I + 1)
                                for i in range(2):
                                    if mode == "plain":
                                        ins = ACT.activation(out=pt[a][i][:], in_=S[a][i][:], func=AF.Exp, scale=0.125)
                                    else:
                                        ins = ACT.activation(out=pt[a][i][:], in_=S[a][i][:], func=AF.Exp, bias=cb[:, info, hcol:hcol + 1])
                                s_p.inc(ins)
                            else:
                                NT = cnt["NT"]
                                DVE.wait_ge(s_qk.h, s_qk.base + I + 1)
                                if NT >= 1:
                                    DVE.wait_ge(s_p.h, s_p.base + cnt["lastnear"] + 1)
                                for i in range(2):
                                    if mode == "tile":
                                        ins = DVE.tensor_tensor(out=tmp[i][:], in0=S[a][i][:], in1=nab[:, i, info, :], op=ALU.add)
                                    else:
                                        for b in range(4):
                                            d = info[b]
                                            cs = slice(b * P, (b + 1) * P)
                                            if abs(d) <= 1:
                                                ins = DVE.tensor_tensor(out=tmp[i][:, cs], in0=S[a][i][:, cs], in1=t5t[:, d + 1, :], op=ALU.add)
                                            else:
                                                sg_ = 0 if d < 0 else 1
                                                ins = DVE.tensor_scalar(out=tmp[i][:, cs], in0=S[a][i][:, cs], scalar1=cb[:, sg_, hcol:hcol + 1],
                                                                        scalar2=None, op0=ALU.add)
                                s_tmp.inc(ins)
                                cnt["NT"] += 1
                                cnt["lastnear"] = I
                                ACT.wait_ge(s_tmp.h, s_tmp.base + cnt["NT"])
                                for i in range(2):
                                    ins = ACT.activation(out=pt[a][i][:], in_=tmp[i][:], func=AF.Exp)
                                s_p.inc(ins)
                            PE.wait_ge(s_p.h, s_p.base + I + 1)
                            for i in range(2):
                                ins = PE.matmul(O[i][:], vt[:, kc, :], pt[a][i][:], start=(ci == 0), stop=(ci == n - 1))
                            ins = PE.matmul(L[1][:], self.ones_b[:], pt[a][1][:], start=(ci == 0), stop=(ci == n - 1))
                            s_pv.inc(ins)
                            for i, eng in ((0, DVE),):
                                eng.wait_ge(s_p.h, s_p.base + I + 1)
                                if ci == 0:
                                    s_l[i].inc(eng.tensor_copy(out=Lacc[i][:], in_=pt[a][i][:]))
                                else:
                                    eng.wait_ge(s_l[i].h, s_l[i].base + I)
                                    s_l[i].inc(eng.tensor_tensor(out=Lacc[i][:], in0=Lacc[i][:], in1=pt[a][i][:], op=ALU.add))
                        cnt["I"] += n
                        qk_end.append(cnt["I"])
                        cnt["QT"] += 1
                        Iend = cnt["I"]
                        FQ = cnt["FQ"]
                        oc = slice(qtl_i * TT, (qtl_i + 1) * TT)
                        PE.wait_ge(s_l[0].h, s_l[0].base + Iend)
                        s_lm.inc(PE.matmul(L[0][:], self.ones_f[:], Lacc[0][:], start=True, stop=True))
                        DVE.wait_ge(s_pv.h, s_pv.base + Iend)
                        DVE.wait_ge(s_lm.h, s_lm.base + FQ + 1)
                        if kind == "A":
                            DVE.reciprocal(out=fr[0][:], in_=L[0][:])
                            self.chain(DVE, DVE.reciprocal(out=fr[1][:], in_=L[1][:]))
                            DVE.tensor_tensor(out=fr[0][:], in0=O[0][:], in1=fr[0][:], op=ALU.mult)
                            self.chain(DVE, DVE.tensor_tensor(out=fr[1][:], in0=O[1][:], in1=fr[1][:], op=ALU.mult))
                            s_f1.inc(DVE.scalar_tensor_tensor(out=fr[2][:], in0=fr[1][:], scalar=neg_lam, in1=fr[0][:], op0=ALU.mult, op1=ALU.add))
                            ACT.wait_ge(s_f1.h, s_f1.base + FQ + 1)
                            s_f2.inc(ACT.activation(out=sqf[:], in_=fr[2][:], func=AF.Square))
                            PE.wait_ge(s_f2.h, s_f2.base + FQ + 1)
                            s_f3.inc(PE.matmul(S[0][0][:], self.onesm[:], sqf[:], start=True, stop=True))
                            ACT.wait_ge(s_f3.h, s_f3.base + FQ + 1)
                            s_f4.inc(ACT.activation(out=fr[0][:], in_=S[0][0][:], func=AF.Sqrt, bias=self.epsc[:, 0:1]))
                            DVE.wait_ge(s_f4.h, s_f4.base + FQ + 1)
                            self.chain(DVE, DVE.reciprocal(out=fr[0][:], in_=fr[0][:]))
                            s_fin.inc(DVE.scalar_tensor_tensor(out=Onv[:, u, oc], in0=fr[2][:], scalar=gsub[:, 0:1], in1=fr[0][:],
                                                               op0=ALU.mult, op1=ALU.mult))
                        else:
                            for i in range(2):
                                ps = slice(64 * i, 64 * i + 64)
                                self.chain(DVE, DVE.reciprocal(out=fr[i][ps, :], in_=L[i][ps, :]))
                                ins = DVE.tensor_tensor(out=Onv[ps, u, oc], in0=O[i][ps, :], in1=fr[i][ps, :], op=ALU.mult)
                            s_fin.inc(ins)
                        cnt["FQ"] += 1
                    pv_end_unit.append(cnt["I"])
                PE.wait_ge(s_fin.h, s_fin.base + cnt["FQ"])
                wo = wos if grp == "s" else wop
                for st_ in range(QR // P):
                    G = cnt["G"]
                    sl = G % 2
                    if grp == "p":
                        r0 = sq * SPq + q0 + st_ * P
                        if G >= 2:
                            SP.wait_ge(st_o[sl].h, st_o[sl].base + 16 * (G // 2))
                        ld_xr[sl].inc(SP.dma_start(out=xr[sl][:], in_=self.xres[r0:r0 + P, :]), 16)
                    else:
                        r0 = q0 + st_ * P
                        if G >= 2:
                            DVE.wait_ge(st_o[sl].h, st_o[sl].base + 16 * (G // 2))
                    for half in range(2):
                        Yc = cnt["Y"]
                        yb = S[Yc % 2][0]
                        if Yc >= 2:
                            PE.wait_ge(s_ye.h, s_ye.base + Yc - 1)
                        for c in range(nu):
                            ins = PE.matmul(yb[:], Onv[:, c, st_ * P:(st_ + 1) * P], wo[:, c, half * TT:(half + 1) * TT], start=(c == 0), stop=(c == nu - 1))
                        s_ym.inc(ins)
                        DVE.wait_ge(s_ym.h, s_ym.base + Yc + 1)
                        hs_ = slice(half * TT, (half + 1) * TT)
                        if grp == "p":
                            if half == 0:
                                DVE.wait_ge(ld_xr[sl].h, ld_xr[sl].base + ld_xr[sl].v - ld_xr[sl].base)
                            s_ye.inc(DVE.tensor_tensor(out=xr[sl][:, hs_], in0=yb[:], in1=xr[sl][:, hs_], op=ALU.add))
                        else:
                            s_ye.inc(DVE.tensor_copy(out=xr[sl][:, hs_], in_=yb[:]))
                        cnt["Y"] += 1
                    POOL.wait_ge(s_ye.h, s_ye.base + cnt["Y"])
                    if grp == "p":
                        dstap = self.xres[r0:r0 + P, :]
                    else:
                        rk_, wi_ = r0 // self.SC, r0 % self.SC
                        dstap = self.ys_in[wi_ // TT][rk_ * TT + wi_ % TT:rk_ * TT + wi_ % TT + P, :]
                    st_o[sl].inc(POOL.dma_start(out=dstap, in_=xr[sl][:]), 16)
                    cnt["G"] += 1
                PE.wait_ge(s_ye.h, s_ye.base + cnt["Y"])
                if grp == "s" and q0 + QR >= SSq and not self.cfg.get("nors"):
                    for sl in range(2):
                        POOL.wait_ge(st_o[sl].h, st_o[sl].v)
                    for kk in range(self.KP):
                        self.cc.inc(POOL.collective_compute("ReduceScatter", ALU.add, replica_groups=[[0, 1, 2, 3], [4, 5, 6, 7]],
                                                            ins=[self.ys_in[kk][:, :]], outs=[self.ys_out[kk][:, :]]))
            for sl in range(2):
                for e in (SP, POOL, ACT, DVE, PE):
                    e.wait_ge(st_o[sl].h, st_o[sl].v)

    def build(self):
        nc = self.nc
        self.declare()
        self.pools = {}
        self.pool_idx = {}
        self.uid = 0
        self.selfsem = {}
        self.pool_i = 0
        self.st_all = Sem(nc, "st_all")
        self.st_all.base = 0
        self.cc = Sem(nc, "cc")
        self.cc.base = 0
        with ExitStack() as es:
            self.setup_consts(es)
            self.new_phase()
            self.t5_setup()
            if self.depth >= 3:
                self.new_phase()
                self.na_setup()
            src = self.x_in
            mp = self.cfg.get("maxphase", 10 ** 9)
            ph = 0
            for li in range(self.depth):
                kind = "ABC"[li % 3]
                j = li // 3
                steps = [lambda: self.ffn_phase(li * 2, src, "mix", li),
                         lambda: self.qkv_phase(kind, j),
                         lambda: self.attn_phase(kind, j, li),
                         lambda: self.add_phase(),
                         lambda: self.ffn_phase(li * 2 + 1, self.xres, "final" if li == self.depth - 1 else None, li)]
                for fstep in steps:
                    if ph >= mp:
                        break
                    self.new_phase()
                    fstep()
                    ph += 1
                src = self.xres
        return nc

    def new_phase(self):
        self.pool_idx = {}

    def sbt(self, name, shape, dt):
        self.uid += 1
        return self.nc.sbuf_tensor(f"{name}_{self.uid}", shape, dt)

    def pst(self, name, shape, dt):
        self.uid += 1
        return self.nc.psum_tensor(f"{name}_{self.uid}", shape, dt)

    def chain(self, eng, ins):
        key = id(eng)
        if key not in self.selfsem:
            self.selfsem[key] = Sem(self.nc, f"self{len(self.selfsem)}")
        s = self.selfsem[key]
        s.inc(ins)
        eng.wait_ge(s.h, s.v)


def _sem_pooled(self, name, kind=None):
    if kind is None:
        n = name.lower()
        if n in ("cst", "wl", "qwl", "awl") or n.startswith(("stx", "sth", "t5st", "nast", "qstq", "qstv", "asto", "addst")):
            kind = "sw"
        elif n.startswith(("ld", "t5ld", "nald", "qld", "aldx", "addld", "wlh")):
            kind = "hw"
        else:
            kind = "c"
    pool = self.pools.setdefault(kind, [])
    i = self.pool_idx.get(kind, 0)
    if i >= len(pool):
        pool.append(Sem(self.nc, f"s{kind}{len(pool)}"))
    s = pool[i]
    self.pool_idx[kind] = i + 1
    s.base = s.v
    return s


Builder.sem = _sem_pooled


def prep_core_inputs(cfg, inp, consts, c):
    SP, SS, NPC = cfg["SP"], cfg["SS"], cfg["NPC"]
    SC = SS // 4
    rho = c % 4
    f = lambda a: np.ascontiguousarray(np.asarray(a, dtype=np.float32))
    xp = np.asarray(inp["x_prompt"])[NPC * c:NPC * (c + 1)].reshape(NPC * SP, D)
    xs = np.asarray(inp["x_sample"])[c // 4, rho * SC:(rho + 1) * SC]
    m = {}
    m["x_in"] = f(np.concatenate([xp, xs], 0))
    dp = cfg["depth"]
    m["ffn_norm"] = f(np.asarray(inp["ffn_norm"]).reshape(8, D)[:2 * dp])
    m["ffn_w_in"] = f(np.asarray(inp["ffn_w_in"]).reshape(8, D, 2 * DFF)[:2 * dp])
    m["ffn_w_out"] = f(np.asarray(inp["ffn_w_out"]).reshape(8, DFF, D)[:2 * dp])
    m["mix_norm"] = f(np.asarray(inp["mix_norm"])[:dp])
    m["final_norm"] = f(np.asarray(inp["final_norm"]).reshape(1, D))
    aw = np.asarray(inp["a_w_qkv"])
    m["a_w_qkv"] = f(aw)
    sl = slice(256 * rho, 256 * rho + 256)
    m["a_w_qkv_s"] = f(np.concatenate([aw[:, :, 0:1024][:, :, sl], aw[:, :, 1024:2048][:, :, sl], aw[:, :, 2048:3072][:, :, sl]], -1))
    m["a_w_out"] = f(inp["a_w_out"])
    m["a_w_out_s"] = f(np.asarray(inp["a_w_out"])[:, sl, :])
    m["a_lambda"] = f(np.asarray(inp["a_lambda"]).reshape(2, 256))
    m["a_subln"] = f(inp["a_subln"])
    t5 = np.asarray(inp["t5_table"])
    m["t5_all"] = f(np.concatenate([t5, t5[:, 2 * rho:2 * rho + 2]], 1))
    bw = np.asarray(inp["b_w_qkv"])[0]
    q, k, v = bw[:, 0:1024], bw[:, 1024:1280], bw[:, 1280:1536]
    dup = lambda a: np.concatenate([np.concatenate([a[:, 64 * g:64 * g + 64]] * 2, 1) for g in range(4)], 1)
    kd, vd = dup(k), dup(v)
    m["b_w_qkv"] = f(np.concatenate([q, kd, vd], 1))
    m["b_w_qkv_s"] = f(np.concatenate([q[:, sl], kd[:, 128 * rho:128 * rho + 128], vd[:, 128 * rho:128 * rho + 128]], 1))
    qn, kn = np.asarray(inp["b_q_norm"])[0], np.asarray(inp["b_k_norm"])[0]
    m["b_qk_norm"] = f(np.stack([np.concatenate([qn, qn]), np.concatenate([kn, kn])]))
    m["b_w_out"] = f(np.asarray(inp["b_w_out"])[0])
    m["b_w_out_s"] = f(np.asarray(inp["b_w_out"])[0][sl])
    cw = np.asarray(inp["c_w_qkv"])[0]
    m["c_w_qkv"] = f(cw)
    m["c_w_qkv_s"] = f(np.concatenate([cw[:, 0:1024][:, sl], cw[:, 1024:2048][:, sl], cw[:, 2048:3072][:, sl]], 1))
    rpb = np.asarray(inp["c_rpb"])[0]
    rall = np.concatenate([rpb, rpb[4 * rho:4 * rho + 4]], 0)
    rT = np.zeros((32, 20, 16), np.float32)
    rT[0:31, :, 0:15] = np.transpose(rall, (2, 0, 1))
    rT[31, :, :] = NEG
    m["c_rpbT"] = f(rT.reshape(32, 320))
    m["c_w_out"] = f(np.asarray(inp["c_w_out"])[0])
    m["c_w_out_s"] = f(np.asarray(inp["c_w_out"])[0][sl])
    m["k_ident"] = consts["ident"]
    m["k_t5oh"] = consts["t5oh"]
    m["k_blk64"] = consts["blk64"]
    m["k_swap"] = consts["swap"]
    m["k_rope"] = consts["rope"]
    m["k_naoh"] = consts["naoh"]
    return m


def run(cfg, inputs):
    consts = make_consts(cfg)
    b = Builder(cfg)
    nc = b.build()
    in_maps = [prep_core_inputs(cfg, inputs, consts, c) for c in range(8)]
    res = run_bass_kernel_spmd(nc, in_maps, core_ids=list(range(8)))
    SP, SS, NPC = cfg["SP"], cfg["SS"], cfg["NPC"]
    SC = SS // 4
    yp = np.zeros((8 * NPC, SP, D), np.float32)
    ys = np.zeros((2, SS, D), np.float32)
    for c in range(8):
        y = np.asarray(res.results[c]["y_out"])
        yp[NPC * c:NPC * (c + 1)] = y[:NPC * SP].reshape(NPC, SP, D)
        ys[c // 4, (c % 4) * SC:(c % 4 + 1) * SC] = y[NPC * SP:]
    return yp, ys


def kernel(**inputs):
    return run(default_cfg(), inputs)
```
